# Optimizing a Trainium2 kernel written in Bass

```python
import math
import jax, jax.numpy as jnp
from jax import lax
import numpy as np

D_MODEL = 2048
BATCH = 32
SEQ = 256
DEPTH = 4
DEC_BATCH = 8
DEC_SEQ = 1024
PAST_LEN = 512

GRID_W = 64
HEAD_DIM = 128
Q_BLOCK = 128
WINDOW = 128
ROPE_THETA = 10000.0
EPS = 1e-6
NEG_INF = -1e30
N_EVEN = (DEPTH + 1) // 2
N_ODD = DEPTH // 2
MIX_W = D_MODEL
HALF_W = MIX_W // 2

A_HEADS = HALF_W // HEAD_DIM
A_KV = A_HEADS // 4
A_GROUP = A_HEADS // A_KV
A_W = A_HEADS * HEAD_DIM
A_KV_W = A_KV * HEAD_DIM
B_W = HALF_W
C_HEADS = HALF_W // (2 * HEAD_DIM)
C_V_DIM = 2 * HEAD_DIM
C_QK_W = 2 * C_HEADS * HEAD_DIM
C_W = C_HEADS * C_V_DIM
D_HEADS = HALF_W // HEAD_DIM
D_KV = D_HEADS // 4
D_GROUP = D_HEADS // D_KV
D_W = D_HEADS * HEAD_DIM
D_KV_W = D_KV * HEAD_DIM
EVEN_SPLIT = (A_W, A_KV_W, A_KV_W, A_W, B_W, B_W, B_W, B_W)
ODD_SPLIT = (C_QK_W, C_QK_W, C_W, C_W, D_W, D_KV_W, D_KV_W, D_W)
EVEN_IN = sum(EVEN_SPLIT)
ODD_IN = sum(ODD_SPLIT)

kernel_name = 'hybrid_diffusion_prefix_trunk_step'


def rms_norm(x, g):
    xf = x.astype(jnp.float32)
    y = xf * lax.rsqrt(jnp.mean(xf * xf, axis=-1, keepdims=True) + EPS)
    return y.astype(x.dtype) * g


def modulation(cond, w, b):
    m = jax.nn.silu(cond) @ w + b
    if m.ndim == 2:
        m = m[:, None, :]
    return jnp.split(m, 3, axis=-1)


def split_cols(p, sizes):
    idx = [int(s) for s in np.cumsum(sizes)[:-1]]
    return jnp.split(p, idx, axis=-1)


def axial_rope_tables(n_tokens):
    rows = n_tokens // GRID_W
    row = jnp.repeat(jnp.arange(rows), GRID_W)
    col = jnp.tile(jnp.arange(GRID_W), rows)
    n_freq = HEAD_DIM // 4
    inv = ROPE_THETA ** (-jnp.arange(n_freq, dtype=jnp.float32) / n_freq)
    ang = jnp.stack([row[:, None] * inv, col[:, None] * inv], axis=1)
    return jnp.cos(ang), jnp.sin(ang)


def apply_rope(x, cos, sin):
    shp = x.shape
    xr = x.reshape(shp[:-1] + (2, 2, HEAD_DIM // 4))
    x1, x2 = xr[..., 0, :], xr[..., 1, :]
    c = cos.astype(x.dtype)
    s = sin.astype(x.dtype)
    out = jnp.stack([x1 * c - x2 * s, x1 * s + x2 * c], axis=-2)
    return out.reshape(shp)


def q_heads(x, n_kv, group):
    b, t, _ = x.shape
    return x.reshape(b, t, n_kv, group, HEAD_DIM).transpose(0, 2, 3, 1, 4)


def kv_heads(x, n_kv):
    b, t, _ = x.shape
    return x.reshape(b, t, n_kv, HEAD_DIM).transpose(0, 2, 1, 3)


def merge_heads(o):
    b, k, g, t, d = o.shape
    return o.transpose(0, 3, 1, 2, 4).reshape(b, t, k * g * d)


def softmax_with_sink(s, sink):
    sk = jnp.broadcast_to(sink, s.shape[:-1] + (1,))
    return jax.nn.softmax(jnp.concatenate([sk, s], axis=-1), axis=-1)[..., 1:]


def attend(q, k, v, sink=None):
    b, nk, g, t, d = q.shape
    nb = t // Q_BLOCK
    qb = jnp.moveaxis(q.reshape(b, nk, g, nb, Q_BLOCK, d), 3, 0)
    scale = d ** -0.5

    def one_block(qi):
        s = jnp.einsum('bkgqd,bksd->bkgqs', qi, k).astype(jnp.float32) * scale
        if sink is None:
            p = jax.nn.softmax(s, axis=-1)
        else:
            p = softmax_with_sink(s, sink.astype(jnp.float32)[None, :, :, None, None])
        return jnp.einsum('bkgqs,bksv->bkgqv', p.astype(v.dtype), v)

    o = lax.map(one_block, qb)
    return jnp.moveaxis(o, 0, 3).reshape(b, nk, g, t, v.shape[-1])


def band_blocks(x, nb):
    b, nk, t, d = x.shape
    xp = jnp.pad(x, ((0, 0), (0, 0), (WINDOW, WINDOW), (0, 0))).reshape(b, nk, nb + 2, WINDOW, d)
    return jnp.concatenate([xp[:, :, :-2], xp[:, :, 1:-1], xp[:, :, 2:]], axis=3)


def banded_attend(q, k, v, k_ctx, v_ctx, sink):
    b, nk, g, t, d = q.shape
    nb = t // WINDOW
    qb = q.reshape(b, nk, g, nb, WINDOW, d)
    kb = band_blocks(k, nb)
    vb = band_blocks(v, nb)
    scale = d ** -0.5
    s_ctx = jnp.einsum('bkgnqd,bksd->bkgnqs', qb, k_ctx).astype(jnp.float32) * scale
    s_band = jnp.einsum('bkgnqd,bknsd->bkgnqs', qb, kb).astype(jnp.float32) * scale
    blk = jnp.arange(nb)[:, None, None]
    qpos = blk * WINDOW + jnp.arange(WINDOW)[None, :, None]
    kpos = (blk - 1) * WINDOW + jnp.arange(3 * WINDOW)[None, None, :]
    valid = (jnp.abs(kpos - qpos) <= WINDOW) & (kpos >= 0) & (kpos < t)
    s_band = jnp.where(valid, s_band, NEG_INF)
    n_ctx = k_ctx.shape[2]
    p = softmax_with_sink(jnp.concatenate([s_ctx, s_band], axis=-1),
                          sink.astype(jnp.float32)[None, :, :, None, None, None]).astype(v.dtype)
    o = (jnp.einsum('bkgnqs,bksv->bkgnqv', p[..., :n_ctx], v_ctx)
         + jnp.einsum('bkgnqs,bknsv->bkgnqv', p[..., n_ctx:], vb))
    return o.reshape(b, nk, g, t, d)


def gated_short_conv(u, gate_b, gate_c, w):
    z = gate_c * u
    zp = jnp.pad(z, ((0, 0), (1, 1), (0, 0)))
    conv = zp[:, :-2] * w[0] + zp[:, 1:-1] * w[1] + zp[:, 2:] * w[2]
    return gate_b * conv


def even_mixer(h, w_in, w_out, qn, kn, sink, conv_w, ctx=None, rope=None):
    qa, ka, va, ga, ub, bb, cb, gb = split_cols(h @ w_in, EVEN_SPLIT)
    q = rms_norm(q_heads(qa, A_KV, A_GROUP), qn)
    k = rms_norm(kv_heads(ka, A_KV), kn)
    v = kv_heads(va, A_KV)
    sink_g = sink.reshape(A_KV, A_GROUP)
    if ctx is None:
        o = attend(q, k, v, sink_g)
        new = (k, v)
    else:
        cos, sin = rope
        o = banded_attend(apply_rope(q, cos, sin), apply_rope(k, cos, sin), v, ctx[0], ctx[1], sink_g)
        new = None
    y_a = merge_heads(o) * jax.nn.silu(ga)
    y_b = gated_short_conv(ub, bb, cb, conv_w) * jax.nn.silu(gb)
    return jnp.concatenate([y_a, y_b], axis=-1) @ w_out, new


def odd_mixer(h, w_in, w_out, cqn, ckn, c_lam, c_on, dqn, dkn, lam_init, ctx=None, rope=None):
    cq, ck, cv, cg, dq, dk, dv, dg = split_cols(h @ w_in, ODD_SPLIT)
    b, t, _ = h.shape
    cq = rms_norm(cq.reshape(b, t, 2, C_HEADS, HEAD_DIM).transpose(0, 2, 3, 1, 4), cqn)
    ck = rms_norm(ck.reshape(b, t, 2, C_HEADS, HEAD_DIM).transpose(0, 2, 3, 1, 4), ckn)
    cv = cv.reshape(b, t, C_HEADS, C_V_DIM).transpose(0, 2, 1, 3)
    q_d = rms_norm(q_heads(dq, D_KV, D_GROUP), dqn)
    k_d = rms_norm(kv_heads(dk, D_KV), dkn)
    v_d = kv_heads(dv, D_KV)
    if ctx is None:
        ck_all, cv_all, dk_all, dv_all = ck, cv, k_d, v_d
        new = (ck, cv, k_d, v_d)
    else:
        cos, sin = rope
        cq = apply_rope(cq, cos, sin)
        q_d = apply_rope(q_d, cos, sin)
        ck_all = jnp.concatenate([ctx[0], apply_rope(ck, cos, sin)], axis=3)
        cv_all = jnp.concatenate([ctx[1], cv], axis=2)
        dk_all = jnp.concatenate([ctx[2], apply_rope(k_d, cos, sin)], axis=2)
        dv_all = jnp.concatenate([ctx[3], v_d], axis=2)
        new = None
    lf = c_lam.astype(jnp.float32)
    lam = jnp.exp(jnp.sum(lf[0] * lf[1])) - jnp.exp(jnp.sum(lf[2] * lf[3])) + lam_init
    o1 = attend(cq[:, 0][:, :, None], ck_all[:, 0], cv_all)[:, :, 0]
    o2 = attend(cq[:, 1][:, :, None], ck_all[:, 1], cv_all)[:, :, 0]
    o_c = rms_norm(o1 - lam.astype(o1.dtype) * o2, c_on) * (1.0 - lam_init)
    y_c = o_c.transpose(0, 2, 1, 3).reshape(b, t, C_W) * jax.nn.silu(cg)
    y_d = merge_heads(attend(q_d, dk_all, dv_all)) * jax.nn.silu(dg)
    return jnp.concatenate([y_c, y_d], axis=-1) @ w_out, new


def setup_inputs(seed: int = 0) -> dict:
    key = jax.random.key(seed)
    ks = jax.random.split(key, 32)

    def nrm(k, shape, scale):
        return jax.random.normal(k, shape, jnp.float32) * scale

    def gain(k, shape):
        return 1.0 + nrm(k, shape, 0.05)

    return {
        'x_prompt': nrm(ks[0], (BATCH, SEQ, D_MODEL), 1.0),
        'x_sample': nrm(ks[1], (DEC_BATCH, DEC_SEQ, D_MODEL), 1.0),
        'cache_a_k': nrm(ks[2], (DEC_BATCH, N_EVEN, A_KV, PAST_LEN, HEAD_DIM), 1.0),
        'cache_a_v': nrm(ks[3], (DEC_BATCH, N_EVEN, A_KV, PAST_LEN, HEAD_DIM), 1.0),
        'cache_c_k': nrm(ks[4], (DEC_BATCH, N_ODD, 2, C_HEADS, PAST_LEN, HEAD_DIM), 1.0),
        'cache_c_v': nrm(ks[5], (DEC_BATCH, N_ODD, C_HEADS, PAST_LEN, C_V_DIM), 1.0),
        'cache_d_k': nrm(ks[6], (DEC_BATCH, N_ODD, D_KV, PAST_LEN, HEAD_DIM), 1.0),
        'cache_d_v': nrm(ks[7], (DEC_BATCH, N_ODD, D_KV, PAST_LEN, HEAD_DIM), 1.0),
        'c': nrm(ks[8], (DEC_BATCH, D_MODEL), 1.0),
        'c_ctx': nrm(ks[9], (D_MODEL,), 1.0),
        'norm_g': gain(ks[10], (DEPTH, D_MODEL)),
        'ada_w': nrm(ks[11], (DEPTH, D_MODEL, 3 * D_MODEL), 0.5 * D_MODEL ** -0.5),
        'ada_b': nrm(ks[12], (DEPTH, 3 * D_MODEL), 0.01),
        'ev_w_in': nrm(ks[13], (N_EVEN, D_MODEL, EVEN_IN), D_MODEL ** -0.5),
        'ev_w_out': nrm(ks[14], (N_EVEN, MIX_W, D_MODEL), MIX_W ** -0.5),
        'a_q_norm': gain(ks[15], (N_EVEN, HEAD_DIM)),
        'a_k_norm': gain(ks[16], (N_EVEN, HEAD_DIM)),
        'a_sink': nrm(ks[17], (N_EVEN, A_HEADS), 0.5),
        'b_conv': nrm(ks[18], (N_EVEN, 3, B_W), 0.5),
        'od_w_in': nrm(ks[19], (N_ODD, D_MODEL, ODD_IN), D_MODEL ** -0.5),
        'od_w_out': nrm(ks[20], (N_ODD, MIX_W, D_MODEL), MIX_W ** -0.5),
        'c_q_norm': gain(ks[21], (N_ODD, HEAD_DIM)),
        'c_k_norm': gain(ks[22], (N_ODD, HEAD_DIM)),
        'c_lambda': nrm(ks[23], (N_ODD, 4, HEAD_DIM), 0.1),
        'c_out_norm': gain(ks[24], (N_ODD, C_V_DIM)),
        'd_q_norm': gain(ks[25], (N_ODD, HEAD_DIM)),
        'd_k_norm': gain(ks[26], (N_ODD, HEAD_DIM)),
    }


def reference(x_prompt, x_sample, cache_a_k, cache_a_v, cache_c_k, cache_c_v, cache_d_k, cache_d_v,
              c, c_ctx, norm_g, ada_w, ada_b, ev_w_in, ev_w_out, a_q_norm, a_k_norm, a_sink, b_conv,
              od_w_in, od_w_out, c_q_norm, c_k_norm, c_lambda, c_out_norm, d_q_norm, d_k_norm):
    rope = axial_rope_tables(x_sample.shape[1])
    xp, xs = x_prompt, x_sample
    a_k, a_v, c_k, c_v, d_k, d_v = [], [], [], [], [], []
    for layer in range(DEPTH):
        sh_p, sc_p, gt_p = modulation(c_ctx, ada_w[layer], ada_b[layer])
        sh_s, sc_s, gt_s = modulation(c, ada_w[layer], ada_b[layer])
        hp = rms_norm(xp, norm_g[layer]) * (1.0 + sc_p) + sh_p
        hs = rms_norm(xs, norm_g[layer]) * (1.0 + sc_s) + sh_s
        i = layer // 2
        if layer % 2 == 0:
            w = (ev_w_in[i], ev_w_out[i], a_q_norm[i], a_k_norm[i], a_sink[i], b_conv[i])
            out_p, (kc, vc) = even_mixer(hp, *w)
            out_s, _ = even_mixer(hs, *w, ctx=(cache_a_k[:, i], cache_a_v[:, i]), rope=rope)
            a_k.append(kc)
            a_v.append(vc)
        else:
            lam_init = 0.8 - 0.6 * math.exp(-0.3 * layer)
            w = (od_w_in[i], od_w_out[i], c_q_norm[i], c_k_norm[i], c_lambda[i], c_out_norm[i],
                 d_q_norm[i], d_k_norm[i], lam_init)
            out_p, (ckc, cvc, dkc, dvc) = odd_mixer(hp, *w)
            out_s, _ = odd_mixer(hs, *w, ctx=(cache_c_k[:, i], cache_c_v[:, i], cache_d_k[:, i], cache_d_v[:, i]),
                                 rope=rope)
            c_k.append(ckc)
            c_v.append(cvc)
            d_k.append(dkc)
            d_v.append(dvc)
        xp = xp + gt_p * out_p
        xs = xs + gt_s * out_s
    return (xp, xs, jnp.stack(a_k, axis=1), jnp.stack(a_v, axis=1), jnp.stack(c_k, axis=1),
            jnp.stack(c_v, axis=1), jnp.stack(d_k, axis=1), jnp.stack(d_v, axis=1))
```

```python
import math
import os
from contextlib import ExitStack

import numpy as np
import concourse.bass as bass
import concourse.mybir as mybir
from concourse.bass_utils import run_bass_kernel_spmd

F32 = mybir.dt.float32
BF16 = mybir.dt.bfloat16
AF = mybir.ActivationFunctionType
ALU = mybir.AluOpType

EPS = 1e-6
NBLK_L = 34
SQ128 = math.sqrt(128.0)


def I(m, *a, **k):
    return (m, a, k)


class Sched:
    ENG = ('pe', 'act', 'dve', 'pool', 'sp')

    def __init__(self, nc):
        self.nc = nc
        self.prog = {e: [] for e in self.ENG}
        self.cnt = {e: 0 for e in self.ENG}
        self.waited = {e: {} for e in self.ENG}
        self.res = {}
        self.slot_cnt = {}
        self.sem_names = list(self.ENG)
        self.final_tokens = []

    def _deps(self, eng, reads, writes):
        deps = []
        for k in reads:
            r = self.res.get(k)
            if r and r['w'] is not None:
                deps.append(('raw', r['w']))
            if r and isinstance(k, tuple) and k[0] == 'ps':
                for t in r['r']:
                    if t[0] != eng:
                        deps.append(('rar', t))
        for k in writes:
            r = self.res.get(k)
            if r:
                if r['w'] is not None:
                    deps.append(('waw', r['w']))
                for t in r['r']:
                    deps.append(('war', t))
        need = {}
        for kind, (sk, val) in deps:
            if sk == eng and eng == 'pe':
                continue
            if self.waited[eng].get(sk, 0) >= val:
                continue
            need[sk] = max(need.get(sk, 0), val)
        for sk, val in need.items():
            self.waited[eng][sk] = val
        return list(need.items())

    def _record(self, token, reads, writes):
        for k in reads:
            r = self.res.setdefault(k, {'w': None, 'r': []})
            r['r'].append(token)
        for k in writes:
            self.res[k] = {'w': token, 'r': []}

    def op(self, eng, ins, reads=(), writes=()):
        if isinstance(ins, tuple):
            ins = [ins]
        waits = self._deps(eng, reads, writes)
        self.cnt[eng] += 1
        token = (eng, self.cnt[eng])
        self.prog[eng].append((waits, ins, eng, 1, False))
        self._record(token, reads, writes)
        return token

    def dma(self, eng, slot, ins, reads=(), writes=(), final=False):
        if isinstance(ins, tuple):
            ins = [ins]
        sk = 'dma:' + str(slot)
        if sk not in self.slot_cnt:
            self.slot_cnt[sk] = 0
            self.sem_names.append(sk)
        waits = self._deps(eng, reads, writes)
        self.slot_cnt[sk] += 16 * len(ins)
        token = (sk, self.slot_cnt[sk])
        self.prog[eng].append((waits, ins, sk, 16, True))
        self._record(token, reads, writes)
        if final:
            self.final_tokens.append(token)
        return token

    def emit(self, stack):
        nc = self.nc
        sems = {}
        for name in self.sem_names:
            sems[name] = stack.enter_context(nc.semaphore(''.join(ch if ch.isalnum() else '_' for ch in name)))
        fin = {}
        for sk, val in self.final_tokens:
            fin[sk] = max(fin.get(sk, 0), val)
        block = stack.enter_context(nc.Block())
        engobj = {'pe': 'tensor', 'act': 'scalar', 'dve': 'vector', 'pool': 'gpsimd', 'sp': 'sync'}

        def make(ename):
            plist = self.prog[ename]

            def body(e):
                for waits, ins, sk, inc, is_dma in plist:
                    for wk, wv in waits:
                        e.wait_ge(sems[wk], wv)
                    last = None
                    for (m, a, k) in ins:
                        last = getattr(e, m)(*a, **k)
                        if is_dma:
                            last.then_inc(sems[sk], inc)
                    if not is_dma:
                        last.then_inc(sems[sk], inc)
                if ename == 'sp':
                    for wk, wv in fin.items():
                        e.wait_ge(sems[wk], wv)
            return body

        for ename in self.ENG:
            getattr(block, engobj[ename])(make(ename))


R_NG = 0
R_AB = 64
R_AQ = 256
R_AK = 258
R_BC = 260
R_CQ = 308
R_CK = 310
R_CL = 312
R_CO = 320
R_DQ = 324
R_DK = 326
R_CC = 328
R_CS = 344
D_AK, D_CK, D_DK, D_CO, D_NL = 0, 2, 4, 8, 12


def even_chunk_order():
    order = [8, 9, 10, 11]
    for j in range(4):
        order += [2 * j, 2 * j + 1, 12 + 2 * j, 13 + 2 * j]
    for m in range(8):
        order += [20 + m, 36 + m, 28 + m, 44 + m]
    return order


def odd_chunk_order():
    order = []
    for h in range(4):
        order += [8 + h, 12 + h, 16 + 2 * h, 17 + 2 * h, h, 4 + h, 24 + 2 * h, 25 + 2 * h]
    order += [40, 41, 42, 43]
    for j in range(4):
        order += [32 + 2 * j, 33 + 2 * j, 44 + 2 * j, 45 + 2 * j]
    return order


def build_program(NL=4):
    nc = bass.Bass("TRN2", target_bir_lowering=False)
    PHASE = int(os.environ.get("MK_PHASE", "5"))
    SKIP = set(os.environ.get("MK_SKIP", "").split(","))

    def din(name, shape):
        return nc.dram_tensor(name, list(shape), F32, kind="ExternalInput").ap()

    def dout(name, shape):
        return nc.dram_tensor(name, list(shape), F32, kind="ExternalOutput").ap()

    x_in = {'P': din("xp", [1024, 2048]), 'S': din("xs", [1024, 2048])}
    ws = din("ws", [4 * NBLK_L, 128, 4096])
    wsa = din("wsa", [4 * 48, 128, 2048])
    pm = din("pm", [384, 128])
    con_d = din("con", [128, 512])
    cs_d = din("cs", [128, 2048])
    sink_d = din("sinkb", [128, 16])
    cak = din("cak", [2, 2, 512, 128])
    cav = din("cav", [2, 2, 512, 128])
    cck = din("cck", [2, 2, 4, 512, 128])
    ccv = din("ccv", [2, 4, 512, 256])
    cdk = din("cdk", [2, 2, 512, 128])
    cdv = din("cdv", [2, 2, 512, 128])
    y_out = {'P': dout("yp", [1024, 2048]), 'S': dout("ys", [1024, 2048])}
    nak = dout("nak", [4, 2, 2, 256, 128])
    nav = dout("nav", [4, 2, 2, 256, 128])
    nck = dout("nck", [4, 2, 2, 4, 256, 128])
    ncv = dout("ncv", [4, 2, 4, 256, 256])
    ndk = dout("ndk", [4, 2, 2, 256, 128])
    ndv = dout("ndv", [4, 2, 2, 256, 128])

    with ExitStack() as st:
        def sb(name, shape, dt):
            return st.enter_context(nc.sbuf_tensor(name, list(shape), dt))

        xT = sb("xT", [128, 16, 1024], F32)
        hT = sb("hT", [128, 16, 1024], BF16)
        yT = sb("yT", [128, 16, 1024], BF16)
        wb = sb("wb", [128, 2, 4096], BF16)
        PT = sb("PT", [128, 384], F32)
        PMs = sb("PMs", [128, 3, 128], F32)
        CON = sb("CON", [128, 512], F32)
        CSN = sb("CSN", [128, 2048], F32)
        SKB = sb("SKB", [128, 16], F32)
        ES = sb("ES", [128, 16], F32)
        DS = sb("DS", [128, 16], F32)
        LT = sb("LT", [128, 8], F32)
        EPSC = sb("EPSC", [128, 4], F32)
        onesb = sb("onesb", [128, 128], BF16)
        onesf = sb("onesf", [128, 128], F32)
        sT = sb("sT", [128, 16, 2], BF16)
        modT = sb("modT", [128, 4, 48, 2], F32)
        MA = sb("MA", [128, 16], F32)
        wa = sb("wa", [128, 2048], BF16)
        KB = sb("KB", [128, 3072], BF16)
        VB = sb("VB", [128, 12, 256], BF16)
        QB = sb("QB", [128, 4, 1024], BF16)
        PTB = sb("PTB", [128, 4, 512], BF16)
        SQB = sb("SQB", [128, 2, 512], BF16)
        BT = sb("BT", [128, 1024], F32)
        ZP = sb("ZP", [128, 1032], F32)
        RSQ = sb("RSQ", [128, 512], F32)
        QG = sb("QG", [128, 512], F32)
        T1 = sb("T1", [128, 512], F32)
        T2 = sb("T2", [128, 512], F32)
        PS = [st.enter_context(nc.psum_tensor("ps%d" % i, [128, 512], F32)) for i in range(8)]

        ident = CON[:, 0:128]
        RT = CON[:, 128:256]
        maskA = CON[:, 256:384]
        maskB = CON[:, 384:512]
        COS = CSN[:, 0:1024]
        SIN = CSN[:, 1024:2048]

        S = Sched(nc)
        FUSE_SS = True
        SSB = [6, 5]
        state = {'gen': 0, 'sc': 0, 'ptb': 0, 'sq': 0, 'stg': 0, 'wi': 0, 'wdma': 0, 'genl': [0, 1, 2, 5, 6, 7],
                 'oalt': 0, 'tset': 0, 'tset_lock': False, 'ai': 0, 'adma': 0, 'pump': 0}

        def gen_bank():
            gl = state['genl']
            state['gen'] = (state['gen'] + 1) % len(gl)
            return gl[state['gen']]

        def sc_bank():
            b = 3 + state['sc']
            state['sc'] = (state['sc'] + 1) % 2
            return b

        def ptb_slot():
            s = state['ptb']
            state['ptb'] = (s + 1) % 4
            return s

        def sq_slot():
            s = state['sq']
            state['sq'] = (s + 1) % 2
            return s

        TSETS = [
            dict(RSQ=(RSQ[:, :], 'RSQ'), QG=(QG[:, :], 'QG'), T1=(T1[:, :], 'T1'), T2=(T2[:, :], 'T2')),
            dict(RSQ=(BT[:, 0:512], ('BT', 0)), QG=(BT[:, 512:1024], ('BT', 1)),
                 T1=(ZP[:, 0:512], ('ZP', 0)), T2=(ZP[:, 516:1028], ('ZP', 1))),
        ]

        def tset():
            if state['tset_lock']:
                return TSETS[0]
            state['tset'] ^= 1
            return TSETS[state['tset']]

        def stg():
            s = state['stg']
            state['stg'] = (s + 1) % 4
            return [TSETS[0]['T1'], TSETS[0]['T2'], TSETS[1]['T1'], TSETS[1]['T2']][s]

        def pk(b):
            return ('ps', b)

        def ACT(out, in_, func, reads, writes, **kw):
            S.op('act', I('activation', out=out, in_=in_, func=func, **kw), reads, writes)

        def TT(out, in0, in1, op, reads, writes, eng='dve'):
            S.op(eng, I('tensor_tensor', out=out, in0=in0, in1=in1, op=op), reads, writes)

        def TS(out, in0, s1, s2, op0, op1, reads, writes):
            if op1 is None:
                S.op('dve', I('tensor_scalar', out=out, in0=in0, scalar1=s1, scalar2=None, op0=op0), reads, writes)
            else:
                S.op('dve', I('tensor_scalar', out=out, in0=in0, scalar1=s1, scalar2=s2, op0=op0, op1=op1), reads, writes)

        def STT(out, in0, scalar, in1, op0, op1, reads, writes):
            S.op('dve', I('scalar_tensor_tensor', out=out, in0=in0, scalar=scalar, in1=in1, op0=op0, op1=op1),
                 reads, writes)

        def CPY(out, in_, reads, writes):
            S.op('dve', I('tensor_copy', out, in_), reads, writes)

        def TRANS4(bank, src_fn, reads):
            S.op('pe', [I('transpose', PS[bank][:, k * 128:(k + 1) * 128], src_fn(k), ident) for k in range(4)],
                 reads + ['CON'], [pk(bank)])

        wlist = []
        for g in ('P', 'S'):
            for l in range(NL):
                wlist += [l * NBLK_L + j for j in range(26)] + [l * NBLK_L + 26 + j for j in range(8)] * 2
        NMOD = NL * 48

        def w_issue():
            i = state['wdma']
            if i >= len(wlist):
                return
            state['wdma'] += 1
            slot = i % 2
            S.dma('pool', 'w%d' % slot, I('dma_start', out=wb[:, slot, :], in_=ws[wlist[i]]), writes=[('wb', slot, 0), ('wb', slot, 1)])

        RING = 32
        ring_bufs = [(hT[:, :, :].rearrange("p a b -> p (a b)"), [('hT', c, T) for c in range(16) for T in range(2)], 'hT'),
                     (yT[:, :, :].rearrange("p a b -> p (a b)"), [('yT', c, T) for c in range(16) for T in range(2)], 'yT')]

        def abuf(i):
            if i < RING:
                buf2d, keys, _ = ring_bufs[(i // 8) % 2]
                return buf2d[:, (i % 8) * 2048:(i % 8 + 1) * 2048], keys
            return wa[:, :], [('wa', 0)]

        def a_issue():
            i = state['adma']
            if i >= NMOD:
                return
            if i < RING:
                state['adma'] += 8
                buf2d, keys, nm = ring_bufs[(i // 8) % 2]
                S.dma('pool', 'ring_' + nm, I('dma_start', out=buf2d.rearrange("p (c f) -> p c f", c=8),
                                              in_=wsa[i:i + 8].rearrange("c p f -> p c f")), writes=keys)
            else:
                state['adma'] += 1
                S.dma('pool', 'wa', I('dma_start', out=wa[:, :], in_=wsa[i]), writes=[('wa', 0)])

        def mod_pump(n):
            for _ in range(n):
                i = state['ai']
                if i >= NMOD:
                    return
                tgt = min((i // 8 + 1) * 8 + 7, RING - 1) if i < RING else i
                while state['adma'] <= min(tgt, NMOD - 1):
                    a_issue()
                state['ai'] += 1
                l, ch = i // 48, i % 48
                buf, keys = abuf(i)
                b = gen_bank()
                S.op('pe', [I('matmul', PS[b][:, 0:2], lhsT=buf[:, hf * 1024 + k8 * 128:hf * 1024 + (k8 + 1) * 128],
                              rhs=sT[:, 8 * hf + k8, :], start=(hf == 0 and k8 == 0), stop=(hf == 1 and k8 == 7))
                            for hf in range(2) for k8 in range(8)], keys + ['sT'], [pk(b)])
                TS(modT[:, l, ch, :], PS[b][:, 0:2], PT[:, R_AB + l * 48 + ch:R_AB + l * 48 + ch + 1],
                   None, ALU.add, None, [pk(b), 'PT'], [('modT', l, ch // 16)])
                if i >= RING - 1:
                    a_issue()

        def w_next():
            i = state['wi']
            state['wi'] += 1
            while state['wdma'] <= min(i + 1, len(wlist) - 1):
                w_issue()
            return i % 2

        S.dma('sp', 'pm', I('dma_start', out=PMs[:, :, :], in_=pm.rearrange("(j p) d -> p j d", p=128)), writes=['PMs'])
        S.dma('sp', 'con', I('dma_start', out=CON[:, :], in_=con_d), writes=['CON'])
        S.dma('sp', 'csn', I('dma_start', out=CSN[:, :], in_=cs_d), writes=['CSN'])
        S.dma('sp', 'skb', I('dma_start', out=SKB[:, :], in_=sink_d), writes=['SKB'])
        for ei, ev in enumerate((2048.0 * EPS, 128.0 * EPS, 256.0 * EPS)):
            S.op('dve', I('memset', EPSC[:, ei:ei + 1], ev), writes=['EPSC'])
        S.op('dve', I('memset', onesb[:, :], 1.0), writes=['onesb'])
        S.op('dve', I('memset', onesf[:, :], 1.0), writes=['onesf'])
        S.op('dve', I('memset', ZP[:, :], 0.0), writes=[('ZP', 0), ('ZP', 1)])
        for j in range(3):
            b = gen_bank()
            S.op('pe', I('transpose', PS[b][:, 0:128], PMs[:, j, :], ident), ['PMs', 'CON'], [pk(b)])
            CPY(PT[:, j * 128:(j + 1) * 128], PS[b][:, 0:128], [pk(b)], ['PT'])
        ACT(ES[:, :], SKB[:, :], AF.Exp, ['SKB'], ['ES'])
        for cond in range(2):
            ACT(sT[:, :, cond], PT[:, R_CC + cond * 16:R_CC + cond * 16 + 16], AF.Silu, ['PT'], ['sT'])
        for (dcol, rcol) in ((D_AK, R_AK), (D_CK, R_CK), (D_DK, R_DK)):
            TS(DS[:, dcol:dcol + 2], PT[:, rcol:rcol + 2], SQ128, None, ALU.mult, None, ['PT'], ['DS'])
        lam_init = [0.8 - 0.6 * math.exp(-0.3 * (2 * i + 1)) for i in range(2)]
        for i in range(2):
            TS(DS[:, D_CO + 2 * i:D_CO + 2 * i + 2], PT[:, R_CO + 2 * i:R_CO + 2 * i + 2],
               16.0 * (1.0 - lam_init[i]), None, ALU.mult, None, ['PT'], ['DS'])
        for i in range(2):
            for r in range(2):
                c0 = R_CL + 4 * i + 2 * r
                TT(LT[:, 2 * i + r:2 * i + r + 1], PT[:, c0:c0 + 1], PT[:, c0 + 1:c0 + 2], ALU.mult, ['PT'], ['LT'])
        b = gen_bank()
        S.op('pe', I('matmul', PS[b][:, 0:4], lhsT=onesf[:, :], rhs=LT[:, 0:4], start=True, stop=True),
             ['onesf', 'LT'], [pk(b)])
        ACT(LT[:, 4:8], PS[b][:, 0:4], AF.Exp, [pk(b)], ['LT2'])
        for i in range(2):
            TT(DS[:, D_NL + i:D_NL + i + 1], LT[:, 5 + 2 * i:6 + 2 * i], LT[:, 4 + 2 * i:5 + 2 * i], ALU.subtract,
               ['LT2'], ['DS'])
            TS(DS[:, D_NL + i:D_NL + i + 1], DS[:, D_NL + i:D_NL + i + 1], -lam_init[i], None, ALU.add, None,
               ['DS'], ['DS'])

        def tsl(T):
            return slice(T * 512, (T + 1) * 512)

        pending = []

        def advance():
            for gsn in list(pending):
                try:
                    next(gsn)
                except StopIteration:
                    pending.remove(gsn)

        def drain():
            while pending:
                advance()

        def spawn(gsn):
            pending.append(gsn)
            try:
                next(gsn)
            except StopIteration:
                pending.remove(gsn)

        def proj_fm(slot, cj, T, src, srckey):
            state['punit'] = state.get('punit', 0) + 1
            if state['punit'] % 2 == 0:
                mod_pump(state['pump'])
            b = gen_bank()
            S.op('pe', [I('matmul', PS[b][:, :], lhsT=wb[:, slot, kc * 256 + cj * 128:kc * 256 + cj * 128 + 128],
                          rhs=src[:, kc, tsl(T)], start=(kc == 0), stop=(kc == 15)) for kc in range(16)],
                 [('wb', slot, 0), ('wb', slot, 1)] + [(srckey, c, T) for c in range(16)], [pk(b)])
            advance()
            return b

        def ones_sum(sq_aps, keys):
            b = gen_bank()
            n = len(sq_aps)
            for i, (ap, k) in enumerate(zip(sq_aps, keys)):
                S.op('pe', I('matmul', PS[b][:, :], lhsT=onesb[:, :], rhs=ap, start=(i == 0), stop=(i == n - 1)),
                     ['onesb', k], [pk(b)])
            return b

        def square_to(src_ap, src_keys):
            s = sq_slot()
            ACT(SQB[:, s, :], src_ap, AF.Square, src_keys, [('SQ', s)])
            return SQB[:, s, :], ('SQ', s)

        def rstd_from(bank, dest, destkey, epsn):
            ecol = {2048.0 * EPS: 0, 128.0 * EPS: 1, 256.0 * EPS: 2}[epsn]
            ACT(dest, PS[bank][:, :], AF.Ln, [pk(bank), 'EPSC'], [destkey], bias=EPSC[:, ecol:ecol + 1], scale=1.0)
            ACT(dest, dest, AF.Exp, [destkey], [destkey], scale=-0.5)

        def qk_epilogue(g, bank, T, gcol, dest, destkey, kout=None):
            spawn(qk_gen(g, bank, T, gcol, dest, destkey, kout))

        def qk_gen(g, bank, T, gcol, dest, destkey, kout):
            ts_ = tset()
            rsq, rsqk = ts_['RSQ']
            qg, qgk = ts_['QG']
            t1, t1k = ts_['T1']
            t2, t2k = ts_['T2']
            sl = sq_slot()
            ACT(SQB[:, sl, :], PS[bank][:, :], AF.Square, [pk(bank)], [('SQ', sl)])
            if g == 'S':
                ACT(qg, PS[bank][:, :], AF.Identity, [pk(bank), "PT", "DS"], [qgk], scale=gcol)
            else:
                TS(qg, PS[bank][:, :], gcol, None, ALU.mult, None, [pk(bank), "PT", "DS"], [qgk])
            yield
            sb_ = ones_sum([SQB[:, sl, :]], [('SQ', sl)])
            rstd_from(sb_, rsq, rsqk, 128.0 * EPS)
            if g == 'S':
                rb = gen_bank()
                S.op('pe', I('matmul', PS[rb][:, :], lhsT=RT, rhs=qg, start=True, stop=True), [qgk, 'CON'], [pk(rb)])
            yield
            if g == 'P':
                if kout is None:
                    TT(dest, qg, rsq, ALU.mult, [qgk, rsqk], [destkey])
                else:
                    TT(qg, qg, rsq, ALU.mult, [qgk, rsqk], [qgk])
                    ACT(dest, qg, AF.Copy, [qgk], [destkey])
                    tb = gen_bank()
                    TRANS4(tb, lambda k: qg[:, k * 128:(k + 1) * 128], [qgk])
                    ACT(t1, PS[tb][:, :], AF.Copy, [pk(tb)], [t1k])
                    if 'kout' not in SKIP:
                        S.dma('sp', 'st_' + str(t1k), [I('dma_start', out=kout[q].rearrange("(s p) d -> p s d", p=128),
                                                       in_=t1[:, q * 256:(q + 1) * 256].rearrange("p (s d) -> p s d", s=2))
                                                     for q in range(2)], reads=[t1k], final=True)
            else:
                TT(t1, qg, COS[:, tsl(T)], ALU.mult, [qgk, 'CSN'], [t1k])
                TT(t2, PS[rb][:, :], SIN[:, tsl(T)], ALU.mult, [pk(rb), 'CSN'], [t2k])
                TT(t1, t1, t2, ALU.add, [t1k, t2k], [t1k])
                TT(dest, t1, rsq, ALU.mult, [t1k, rsqk], [destkey])

        def load_ctx_k(src_list):
            if 'ctxk' in SKIP:
                return
            for (src, hm) in src_list:
                stage, sk = stg()
                S.dma('sp', 'ld_' + str(sk), I('dma_start', out=stage.rearrange("p (c d) -> p c d", d=128),
                                          in_=src.rearrange("(c p) d -> p c d", p=128)), writes=[sk])
                tb = gen_bank()
                TRANS4(tb, lambda k, stage=stage: stage[:, k * 128:(k + 1) * 128], [sk])
                ACT(KB[:, hm * 1536:hm * 1536 + 512], PS[tb][:, :], AF.Copy, [pk(tb)], [('KB', hm, 'ctx')])

        def load_ctx_v(src_list):
            if 'ctxv' in SKIP:
                return
            S.dma('pool', 'vbctx', [I('dma_start', out=VB[:, 0:4, c0:c0 + w], in_=src.rearrange("(c p) d -> p c d", p=128))
                                    for (src, c0, w) in src_list], writes=[('VB', 'ctx')])

        def proj_v(g, slot, vout_fn):
            c0 = 4 if g == 'S' else 0
            for u in range(4):
                b = gen_bank()
                ins = []
                for s2 in range(2):
                    ts_ = 2 * u + s2
                    for kc in range(16):
                        ins.append(I('matmul', PS[b][:, s2 * 256:(s2 + 1) * 256], lhsT=hT[:, kc, ts_ * 128:(ts_ + 1) * 128],
                                     rhs=wb[:, slot, kc * 256:(kc + 1) * 256], start=(s2 == 0 and kc == 0),
                                     stop=(kc == 15), skip_group_check=True))
                S.op('pe', ins, [('wb', slot, 0), ('wb', slot, 1)] + [('hT', c, u // 2) for c in range(16)], [pk(b)])
                advance()
                ACT(VB[:, c0 + 2 * u:c0 + 2 * u + 2, :], PS[b][:, :].rearrange("p (s v) -> p s v", s=2), AF.Copy,
                    [pk(b)], [('VB', 'own', u)])
                if g == 'P':
                    stage, sk = stg()
                    CPY(stage, PS[b][:, :], [pk(b)], [sk])
                    vout_fn(u, stage, sk)

        def attention(units, nv, q_of, evac):
            if 'attn' in SKIP:
                return
            drain()
            state['oalt'] ^= 1
            if nv == 1:
                state['dalt'] = state.get('dalt', 0) ^ 1
                den_b = [0, 1][state['dalt']]
                ob = [5, 7][state['oalt']:state['oalt'] + 1]
                sc_list = [3, 4, 2, 6]
            else:
                ob = [[5, 6], [7, 2]][state['oalt']]
                den_b = 0
                sc_list = [3, 4, 1]
            LA = len(sc_list) - 1
            sc = [None] * len(units)

            def subs_of(u):
                return u.get('subs') or [(u['k'], u['vkc'], u['lo'], u['hi'])]

            def scores(i):
                u = units[i]
                b = sc_list[i % len(sc_list)]
                sc[i] = b
                S.op('pe', [I('matmul', PS[b][:, plo:phi], lhsT=kap, rhs=q_of(u), start=(si == 0), stop=True,
                              skip_group_check=True) for si, (kap, _, plo, phi) in enumerate(subs_of(u))],
                     u['kkeys'] + u['qkeys'], [pk(b)])
            for i in range(min(LA, len(units))):
                scores(i)
            for i, u in enumerate(units):
                if i + LA < len(units):
                    scores(i + LA)
                b = sc[i]
                p = ptb_slot()
                lo, hi = u['lo'], u['hi']
                subs = subs_of(u)
                elo, ehi = min(x[2] for x in subs), max(x[3] for x in subs)
                ACT(PTB[:, p, elo:ehi], PS[b][:, elo:ehi], AF.Exp, [pk(b)], [('PTB', p)])
                for (col, mk) in u['masks']:
                    TT(PTB[:, p, col:col + 128], PTB[:, p, col:col + 128], mk, ALU.mult, [('PTB', p), 'CON'], [('PTB', p)])
                ins = []
                for si, (_, vkc, plo, phi) in enumerate(subs):
                    stf = (i == 0 and si == 0)
                    for j in range(nv):
                        ins.append(I('matmul', PS[ob[j]][:, lo:hi],
                                     lhsT=VB[:, vkc, u['vc0'] + j * 128:u['vc0'] + (j + 1) * 128],
                                     rhs=PTB[:, p, plo:phi], start=stf, stop=False, skip_group_check=True))
                    ins.append(I('matmul', PS[den_b][:, lo:hi], lhsT=onesb[:, :], rhs=PTB[:, p, plo:phi],
                                 start=stf, stop=False, skip_group_check=True))
                S.op('pe', ins, [('PTB', p), 'onesb'] + u['vkeys'], [pk(ob[j]) for j in range(nv)] + [pk(den_b)])
            evac(den_b, ob)

        def attention_pair(calls):
            if 'attn' in SKIP:
                return
            drain()
            sc_rot = [3, 4, 2, 6]
            items = []
            nmax = max(len(c['units']) for c in calls)
            for i in range(nmax):
                for k, c in enumerate(calls):
                    if i < len(c['units']):
                        items.append((k, i))
            assert len(items) <= 4
            scb, pslot = {}, {}
            for n_, (k, i) in enumerate(items):
                c = calls[k]
                u = c['units'][i]
                b = sc_rot[n_ % 4]
                scb[(k, i)] = b
                subs = u.get('subs') or [(u['k'], u['vkc'], u['lo'], u['hi'])]
                S.op('pe', [I('matmul', PS[b][:, plo:phi], lhsT=kap, rhs=c['q_of'](u), start=(si == 0), stop=True,
                              skip_group_check=True) for si, (kap, _, plo, phi) in enumerate(subs)],
                     u['kkeys'] + u['qkeys'], [pk(b)])
            for (k, i) in items:
                u = calls[k]['units'][i]
                b = scb[(k, i)]
                p = ptb_slot()
                pslot[(k, i)] = p
                subs = u.get('subs') or [(u['k'], u['vkc'], u['lo'], u['hi'])]
                elo, ehi = min(x[2] for x in subs), max(x[3] for x in subs)
                ACT(PTB[:, p, elo:ehi], PS[b][:, elo:ehi], AF.Exp, [pk(b)], [('PTB', p)])
            for (k, i) in items:
                u = calls[k]['units'][i]
                p = pslot[(k, i)]
                ob, den_b = [5, 7][k], [0, 1][k]
                lo, hi = u['lo'], u['hi']
                subs = u.get('subs') or [(u['k'], u['vkc'], u['lo'], u['hi'])]
                ins = []
                for si, (_, vkc, plo, phi) in enumerate(subs):
                    stf = (i == 0 and si == 0)
                    ins.append(I('matmul', PS[ob][:, lo:hi], lhsT=VB[:, vkc, u['vc0']:u['vc0'] + 128],
                                 rhs=PTB[:, p, plo:phi], start=stf, stop=False, skip_group_check=True))
                    ins.append(I('matmul', PS[den_b][:, lo:hi], lhsT=onesb[:, :], rhs=PTB[:, p, plo:phi],
                                 start=stf, stop=False, skip_group_check=True))
                S.op('pe', ins, [('PTB', p), 'onesb'] + u['vkeys'], [pk(ob), pk(den_b)])
            for k, c in enumerate(calls):
                c['evac']([0, 1][k], [[5, 7][k]])

        def dense_units(g, T, khm, vc0, qkeys):
            units = []
            if g == 'S':
                for kc in range(12):
                    kk = [('KB', khm, 'ctx')] if kc < 4 else [('KB', khm, (kc - 4) // 4)]
                    vk = [('VB', 'ctx')] if kc < 4 else [('VB', 'own', (kc - 4) // 2)]
                    units.append(dict(k=KB[:, khm * 1536 + kc * 128:khm * 1536 + (kc + 1) * 128], vkc=kc, vc0=vc0,
                                      lo=0, hi=512, masks=[], kkeys=kk, vkeys=vk, qkeys=qkeys, qlo=T * 512))
            else:
                for s2 in range(2):
                    seq = 2 * T + s2
                    subs = [(KB[:, khm * 1536 + (2 * seq + kc2) * 128:khm * 1536 + (2 * seq + kc2 + 1) * 128],
                             2 * seq + kc2, kc2 * 256, (kc2 + 1) * 256) for kc2 in range(2)]
                    units.append(dict(subs=subs, vc0=vc0, lo=s2 * 256, hi=(s2 + 1) * 256, masks=[],
                                      kkeys=[('KB', khm, (2 * seq) // 4)], vkeys=[('VB', 'own', seq)],
                                      qkeys=qkeys, qlo=T * 512 + s2 * 256))
            return units

        def band_units(T, khm, vc0, qkeys):
            units = []
            for kc in range(4):
                units.append(dict(k=KB[:, khm * 1536 + kc * 128:khm * 1536 + (kc + 1) * 128], vkc=kc, vc0=vc0,
                                  lo=0, hi=512, masks=[], kkeys=[('KB', khm, 'ctx')], vkeys=[('VB', 'ctx')],
                                  qkeys=qkeys, qlo=T * 512))
            for kb in range(max(0, 4 * T - 1), min(7, 4 * T + 4) + 1):
                qlo = max(kb - 1, 4 * T)
                qhi = min(kb + 1, 4 * T + 3)
                lo = (qlo - 4 * T) * 128
                hi = (qhi - 4 * T + 1) * 128
                masks = []
                if qlo <= kb + 1 <= qhi:
                    masks.append(((kb + 1 - 4 * T) * 128, maskA))
                if qlo <= kb - 1 <= qhi:
                    masks.append(((kb - 1 - 4 * T) * 128, maskB))
                units.append(dict(k=KB[:, khm * 1536 + 512 + kb * 128:khm * 1536 + 512 + (kb + 1) * 128],
                                  vkc=4 + kb, vc0=vc0, lo=lo, hi=hi, masks=masks,
                                  kkeys=[('KB', khm, kb // 4)], vkeys=[('VB', 'own', kb // 2)],
                                  qkeys=qkeys, qlo=T * 512 + lo))
            return units

        def gqa_mixer(g, l, kcache, vcache, gq_col, gk_col, ybase, nkout, nvout, sink_base, banded):
            state['genl'] = [0, 1, 2, 6]
            state['tset_lock'] = False
            li = l // 2
            koff = 512 if g == 'S' else 0
            slot = w_next()
            for T in range(2):
                for hm in range(2):
                    b = proj_fm(slot, hm, T, hT, 'hT')
                    kout = nkout[2 * T:2 * T + 2, li, hm] if g == 'P' else None
                    qk_epilogue(g, b, T, gk_col, KB[:, hm * 1536 + koff + T * 512:hm * 1536 + koff + (T + 1) * 512],
                                ('KB', hm, T), kout=kout)
            slot = w_next()
            if g == 'S':
                load_ctx_k([(kcache[li, hm], hm) for hm in range(2)])
                load_ctx_v([(vcache[li, hm], hm * 128, 128) for hm in range(2)])

            def vout(u, stage, sk):
                if 'vout' not in SKIP:
                  S.dma('sp', 'st_' + str(sk), [I('dma_start', out=nvout[u, li, hh].rearrange("(s p) d -> p s d", p=128),
                                           in_=stage.rearrange("p (s h d) -> p s h d", s=2, h=2)[:, :, hh, :])
                                         for hh in range(2)], reads=[sk], final=True)
            proj_v(g, slot, vout)
            for j in range(4):
                slot = w_next()
                qs = (j % 2) * 2
                for cj in range(2):
                    for T in range(2):
                        b = proj_fm(slot, cj, T, hT, 'hT')
                        qk_epilogue(g, b, T, gq_col, QB[:, qs + cj, tsl(T)], ('QB', qs + cj, T))
                slot = w_next()
                for cj in range(2):
                    for T in range(2):
                        b = proj_fm(slot, cj, T, hT, 'hT')
                        ych = ybase + 2 * j + cj
                        ACT(yT[:, ych, tsl(T)], PS[b][:, :], AF.Silu, [pk(b)], [('yT', ych, T)])
                for cj in range(2):
                    h = 2 * j + cj
                    khm = h // 4
                    qslot = qs + cj
                    pair = []
                    for T in range(2):
                        qkeys = [('QB', qslot, T)]
                        if banded:
                            units = band_units(T, khm, khm * 128, qkeys)
                        else:
                            units = dense_units(g, T, khm, khm * 128, qkeys)

                        def q_of(u, qslot=qslot):
                            return QB[:, qslot, u['qlo']:u['qlo'] + (u['hi'] - u['lo'])]

                        def evac(den_b, ob, h=h, T=T):
                            ts_ = tset()
                            rsq, rsqk = ts_['RSQ']
                            t1, t1k = ts_['T1']
                            if sink_base is not None:
                                ACT(rsq, PS[den_b][:, :], AF.Ln, [pk(den_b), 'ES'], [rsqk],
                                    bias=ES[:, sink_base + h:sink_base + h + 1], scale=1.0)
                            else:
                                ACT(rsq, PS[den_b][:, :], AF.Ln, [pk(den_b)], [rsqk])
                            ACT(rsq, rsq, AF.Exp, [rsqk], [rsqk], scale=-1.0)
                            TT(t1, PS[ob[0]][:, :], rsq, ALU.mult, [pk(ob[0]), rsqk], [t1k])
                            ych = ybase + h
                            TT(yT[:, ych, tsl(T)], t1, yT[:, ych, tsl(T)], ALU.mult, [t1k, ('yT', ych, T)],
                               [('yT', ych, T)])
                        if g == 'P':
                            pair.append(dict(units=units, q_of=q_of, evac=evac))
                        else:
                            attention(units, 1, q_of, evac)
                    if g == 'P':
                        attention_pair(pair)

        def conv_mixer(g, l):
            li = l // 2
            z4 = ZP[:, 0:1032].rearrange("p (s l) -> p s l", l=258)
            drain()
            state['genl'] = [0, 1, 2, 5, 6, 7]
            S.op('dve', I('memset', ZP[:, :], 0.0), [], [('ZP', 0), ('ZP', 1)])
            for m in range(8):
                slot = w_next()
                for T in range(2):
                    b = proj_fm(slot, 0, T, hT, 'hT')
                    ACT(BT[:, tsl(T)], PS[b][:, :], AF.Copy, [pk(b)], [('BT', T)])
                for T in range(2):
                    b = proj_fm(slot, 1, T, hT, 'hT')
                    if g == 'P':
                        TT(z4[:, 2 * T:2 * T + 2, 1:257], PS[b][:, :].rearrange("p (s l) -> p s l", s=2),
                           BT[:, tsl(T)].rearrange("p (s l) -> p s l", s=2), ALU.mult, [pk(b), ('BT', T)], [('ZP', T)])
                    else:
                        TT(ZP[:, 1 + T * 512:1 + (T + 1) * 512], PS[b][:, :], BT[:, tsl(T)], ALU.mult,
                           [pk(b), ('BT', T)], [('ZP', 0)] if T == 0 else [('ZP', 0), ('ZP', 1)])
                wc = [PT[:, R_BC + li * 24 + tap * 8 + m:R_BC + li * 24 + tap * 8 + m + 1] for tap in range(3)]
                for T in range(2):
                    if g == 'P':
                        zs = [z4[:, 2 * T:2 * T + 2, tap:tap + 256] for tap in range(3)]
                        acc = BT[:, tsl(T)].rearrange("p (s l) -> p s l", s=2)
                        zkeys = [('ZP', T)]
                    else:
                        zs = [ZP[:, T * 512 + tap:T * 512 + tap + 512] for tap in range(3)]
                        acc = BT[:, tsl(T)]
                        zkeys = [('ZP', 0), ('ZP', 1)]
                    ACT(acc, zs[0], AF.Identity, zkeys + ['PT'], [('BT', T)], scale=wc[0])
                    for tap in (1, 2):
                        STT(acc, zs[tap], wc[tap], acc, ALU.mult, ALU.add, zkeys + ['PT', ('BT', T)], [('BT', T)])
                slot = w_next()
                for T in range(2):
                    b = proj_fm(slot, 0, T, hT, 'hT')
                    TT(BT[:, tsl(T)], BT[:, tsl(T)], PS[b][:, :], ALU.mult, [pk(b), ('BT', T)], [('BT', T)])
                for T in range(2):
                    b = proj_fm(slot, 1, T, hT, 'hT')
                    ych = 8 + m
                    ACT(yT[:, ych, tsl(T)], PS[b][:, :], AF.Silu, [pk(b)], [('yT', ych, T)])
                    TT(yT[:, ych, tsl(T)], BT[:, tsl(T)], yT[:, ych, tsl(T)], ALU.mult, [('BT', T), ('yT', ych, T)],
                       [('yT', ych, T)])

        def diff_mixer(g, l):
            state['genl'] = [0, 1, 2, 7]
            state['tset_lock'] = False
            li = l // 2
            koff = 512 if g == 'S' else 0
            gq_c = PT[:, R_CQ + li:R_CQ + li + 1]
            gk_c = DS[:, D_CK + li:D_CK + li + 1]
            for h in range(4):
                slot = w_next()
                for mp in range(2):
                    for T in range(2):
                        b = proj_fm(slot, mp, T, hT, 'hT')
                        kout = nck[2 * T:2 * T + 2, li, mp, h] if g == 'P' else None
                        qk_epilogue(g, b, T, gk_c, KB[:, mp * 1536 + koff + T * 512:mp * 1536 + koff + (T + 1) * 512],
                                    ('KB', mp, T), kout=kout)
                slot = w_next()
                if g == 'S':
                    load_ctx_k([(cck[li, mp, h], mp) for mp in range(2)])
                    load_ctx_v([(ccv[li, h], 0, 256)])

                def vout(u, stage, sk, h=h):
                    S.dma('sp', 'st_' + str(sk), I('dma_start', out=ncv[u, li, h].rearrange("(s p) v -> p s v", p=128),
                                              in_=stage.rearrange("p (s v) -> p s v", s=2)),
                          reads=[sk], final=True)
                proj_v(g, slot, vout)
                slot = w_next()
                qs = (h % 2) * 2
                for mp in range(2):
                    for T in range(2):
                        b = proj_fm(slot, mp, T, hT, 'hT')
                        qk_epilogue(g, b, T, gq_c, QB[:, qs + mp, tsl(T)], ('QB', qs + mp, T))
                slot = w_next()
                for cj in range(2):
                    for T in range(2):
                        b = proj_fm(slot, cj, T, hT, 'hT')
                        ych = 2 * h + cj
                        ACT(yT[:, ych, tsl(T)], PS[b][:, :], AF.Silu, [pk(b)], [('yT', ych, T)])
                AOB = [[(BT[:, 0:512], ('BT', 0)), (BT[:, 512:1024], ('BT', 1))],
                       [(ZP[:, 0:512], ('ZP', 0)), (ZP[:, 516:1028], ('ZP', 1))]]

                def run_attn(T, mp, h=h, qs=qs):
                    units = dense_units(g, T, mp, 0, [('QB', qs + mp, T)])

                    def q_of(u, qslot=qs + mp):
                        return QB[:, qslot, u['qlo']:u['qlo'] + (u['hi'] - u['lo'])]

                    def evac(den_b, ob):
                        ACT(RSQ[:, :], PS[den_b][:, :], AF.Ln, [pk(den_b)], ['RSQ'])
                        ACT(RSQ[:, :], RSQ[:, :], AF.Exp, ['RSQ'], ['RSQ'], scale=-1.0)
                        if mp == 0:
                            for jv in range(2):
                                ao, aok = AOB[T][jv]
                                TT(ao, PS[ob[jv]][:, :], RSQ[:, :], ALU.mult, [pk(ob[jv]), 'RSQ'], [aok])
                        else:
                            TS(RSQ[:, :], RSQ[:, :], DS[:, D_NL + li:D_NL + li + 1], None, ALU.mult, None,
                               ['RSQ', 'DS'], ['RSQ'])
                            for jv in range(2):
                                ao, aok = AOB[T][jv]
                                TT(T1[:, :], PS[ob[jv]][:, :], RSQ[:, :], ALU.mult, [pk(ob[jv]), 'RSQ'], ['T1'])
                                TT(ao, ao, T1[:, :], ALU.add, ['T1', aok], [aok])
                    attention(units, 2, q_of, evac)

                def final_gen(T, h=h):
                    sqa, sqk = [], []
                    for jv in range(2):
                        ao, aok = AOB[T][jv]
                        a_, k_ = square_to(ao, [aok])
                        sqa.append(a_)
                        sqk.append(k_)
                    yield
                    sbk = ones_sum(sqa, sqk)
                    rstd_from(sbk, RSQ[:, :], 'RSQ', 256.0 * EPS)
                    for jv in range(2):
                        ao, aok = AOB[T][jv]
                        ych = 2 * h + jv
                        STT(T1[:, :], ao, DS[:, D_CO + 2 * li + jv:D_CO + 2 * li + jv + 1],
                            RSQ[:, :], ALU.mult, ALU.mult, [aok, 'DS', 'RSQ'], ['T1'])
                        TT(yT[:, ych, tsl(T)], T1[:, :], yT[:, ych, tsl(T)], ALU.mult, ['T1', ('yT', ych, T)],
                           [('yT', ych, T)])

                run_attn(0, 0)
                run_attn(0, 1)
                run_attn(1, 0)
                for _ in final_gen(0):
                    pass
                run_attn(1, 1)
                spawn(final_gen(1))

        for g in ('P', 'S'):
            cond = 0 if g == 'P' else 1
            state['genl'] = [0, 1, 2, 5, 6, 7]
            if g == 'P':
                while state['adma'] < min(16, NMOD):
                    a_issue()
            for ts_ in range(8):
                for cb in range(4):
                    if g == 'P' and ts_ * 4 + cb >= 8:
                        mod_pump(2 if state['ai'] < RING else 1)
                    stage, sk = stg()
                    S.dma('sp', 'ld_' + str(sk), I('dma_start', out=stage,
                                              in_=x_in[g][ts_ * 128:(ts_ + 1) * 128, cb * 512:(cb + 1) * 512]), writes=[sk])
                    tb = gen_bank()
                    TRANS4(tb, lambda k, stage=stage: stage[:, k * 128:(k + 1) * 128], [sk])
                    outap = xT[:, cb * 4:(cb + 1) * 4, ts_ * 128:(ts_ + 1) * 128]
                    inap = PS[tb][:, :].rearrange("p (k t) -> p k t", k=4)
                    wr = [('xT', c, ts_ // 4) for c in range(cb * 4, cb * 4 + 4)]
                    if cb % 2 == 0:
                        ACT(outap, inap, AF.Copy, [pk(tb)], wr)
                    else:
                        CPY(outap, inap, [pk(tb)], wr)

            for l in range(NL):
                li = l // 2
                even = (l % 2 == 0)
                if g == 'P':
                    state['pump'] = 0
                    mod_pump(max(0, l * 48 + 32 - state['ai']))
                    state['pump'] = 1
                if PHASE < 2:
                    continue
                def prep_norm(l2):
                    TS(MA[:, :], modT[:, l2, 16:32, cond], 1.0, math.sqrt(2048.0), ALU.add, ALU.mult, [('modT', l2, 1)], ['MA'])
                    TT(MA[:, :], MA[:, :], PT[:, R_NG + l2 * 16:R_NG + l2 * 16 + 16], ALU.mult, ['MA', 'PT'], ['MA'])

                def norm_tile(l2, T, fused):
                    if fused:
                        b = SSB[T]
                    else:
                        b = gen_bank()
                        for c in range(16):
                            s = sq_slot()
                            if c % 2 == 0:
                                ACT(SQB[:, s, :], xT[:, c, tsl(T)], AF.Square, [('xT', c, T)], [('SQ', s)])
                            else:
                                TT(SQB[:, s, :], xT[:, c, tsl(T)], xT[:, c, tsl(T)], ALU.mult, [('xT', c, T)], [('SQ', s)])
                            S.op('pe', I('matmul', PS[b][:, :], lhsT=onesb[:, :], rhs=SQB[:, s, :], start=(c == 0), stop=(c == 15)),
                                 ['onesb', ('SQ', s)], [pk(b)])
                    rstd_from(b, RSQ[:, :], 'RSQ', 2048.0 * EPS)
                    for c in range(16):
                        stage, sk = stg()
                        STT(stage, xT[:, c, tsl(T)], MA[:, c:c + 1], RSQ[:, :], ALU.mult, ALU.mult,
                            [('xT', c, T), 'MA', 'RSQ'], [sk])
                        ACT(hT[:, c, tsl(T)], stage, AF.Identity, [sk, ('modT', l2, 0)], [('hT', c, T)],
                            bias=modT[:, l2, c, cond:cond + 1], scale=1.0)

                if l == 0:
                    prep_norm(0)
                    norm_tile(0, 0, False)
                    norm_tile(0, 1, False)

                if PHASE < 3:
                    continue
                if even:
                    gqa_mixer(g, l, cak, cav, PT[:, R_AQ + li:R_AQ + li + 1], DS[:, D_AK + li:D_AK + li + 1], 0,
                              nak, nav, 8 * li, banded=(g == 'S'))
                    if PHASE >= 4:
                        conv_mixer(g, l)
                else:
                    diff_mixer(g, l)
                    gqa_mixer(g, l, cdk, cdv, PT[:, R_DQ + li:R_DQ + li + 1], DS[:, D_DK + li:D_DK + li + 1], 8,
                              ndk, ndv, None, banded=False)

                drain()
                if PHASE < 5:
                    continue
                fuse = FUSE_SS and (l + 1 < NL)
                state['genl'] = [0, 1, 2, 7] if fuse else [0, 1, 2, 5, 6, 7]
                state['tset_lock'] = False
                if g == 'P':
                    mod_pump(max(0, (l + 1) * 48 - state['ai']))

                def ss_gen(mch, T):
                    sl = sq_slot()
                    if mch % 2 == 0:
                        ACT(SQB[:, sl, :], xT[:, mch, tsl(T)], AF.Square, [('xT', mch, T)], [('SQ', sl)])
                    else:
                        TT(SQB[:, sl, :], xT[:, mch, tsl(T)], xT[:, mch, tsl(T)], ALU.mult, [('xT', mch, T)], [('SQ', sl)])
                    yield
                    S.op('pe', I('matmul', PS[SSB[T]][:, :], lhsT=onesb[:, :], rhs=SQB[:, sl, :],
                                 start=(mch == 0), stop=(mch == 15), skip_group_check=True),
                         ['onesb', ('SQ', sl)], [pk(SSB[T])])

                def norm_gen(l2, T):
                    yield
                    if T == 0:
                        if g == 'P':
                            mod_pump(max(0, l2 * 48 + 32 - state['ai']))
                        prep_norm(l2)
                    norm_tile(l2, T, True)

                for T in range(2):
                    for jb in range(8):
                        slot = w_next()
                        for cj in range(2):
                            mch = 2 * jb + cj
                            b = proj_fm(slot, cj, T, yT, 'yT')
                            STT(xT[:, mch, tsl(T)], PS[b][:, :], modT[:, l, 32 + mch, cond:cond + 1], xT[:, mch, tsl(T)],
                                ALU.mult, ALU.add, [pk(b), ('modT', l, 2), ('xT', mch, T)], [('xT', mch, T)])
                            if fuse:
                                spawn(ss_gen(mch, T))
                    if fuse:
                        spawn(norm_gen(l + 1, T))
                    else:
                        drain()

            for ts_ in range(8):
                for cb in range(4):
                    tb = gen_bank()
                    TRANS4(tb, lambda k, ts_=ts_, cb=cb: xT[:, cb * 4 + k, ts_ * 128:(ts_ + 1) * 128],
                           [('xT', cb * 4 + k, ts_ // 4) for k in range(4)])
                    stage, sk = stg()
                    if cb % 2 == 0:
                        ACT(stage, PS[tb][:, :], AF.Copy, [pk(tb)], [sk])
                    else:
                        CPY(stage, PS[tb][:, :], [pk(tb)], [sk])
                    S.dma('sp', 'st_' + str(sk), I('dma_start', out=y_out[g][ts_ * 128:(ts_ + 1) * 128, cb * 512:(cb + 1) * 512],
                                              in_=stage), reads=[sk], final=True)

        S.emit(st)
    return nc


def _to_blocks(W):
    nb = W.shape[1] // 256
    return np.ascontiguousarray(W.reshape(16, 128, nb, 256).transpose(2, 1, 0, 3)).reshape(nb, 128, 4096)


def _chunk_cols(order):
    return np.concatenate([np.arange(c * 128, (c + 1) * 128) for c in order])


def _host_constants():
    con = np.zeros((128, 512), np.float32)
    con[:, 0:128] = np.eye(128, dtype=np.float32)
    R = np.zeros((128, 128), np.float32)
    for p in range(128):
        if p % 64 < 32:
            R[p, p + 32] = -1.0
        else:
            R[p, p - 32] = 1.0
    con[:, 128:256] = R.T
    jj = np.arange(128)[:, None]
    ii = np.arange(128)[None, :]
    con[:, 256:384] = (jj >= ii).astype(np.float32)
    con[:, 384:512] = (jj <= ii).astype(np.float32)
    t = np.arange(1024)
    row = (t // 64).astype(np.float32)
    col = (t % 64).astype(np.float32)
    inv = (np.float32(10000.0) ** (-np.arange(32, dtype=np.float32) / np.float32(32))).astype(np.float32)
    cs = np.zeros((128, 2048), np.float32)
    for p in range(128):
        pos = row if p < 64 else col
        ang = (pos * inv[p % 32]).astype(np.float32)
        cs[p, 0:1024] = np.cos(ang)
        cs[p, 1024:2048] = np.sin(ang)
    return con, cs


_CACHE = {}


def kernel(**inp):
    NL = int(os.environ.get("MK_NL", "4"))
    f = lambda k: np.asarray(inp[k], dtype=np.float32)
    ws = np.empty((4 * NBLK_L, 128, 4096), np.float32)
    wsa = np.empty((4 * 48, 128, 2048), np.float32)
    ada_w = f('ada_w')
    ev_in, ev_out, od_in, od_out = f('ev_w_in'), f('ev_w_out'), f('od_w_in'), f('od_w_out')
    ecols, ocols = _chunk_cols(even_chunk_order()), _chunk_cols(odd_chunk_order())
    for l in range(4):
        i = l // 2
        base = l * NBLK_L
        wsa[l * 48:(l + 1) * 48] = ada_w[l].reshape(2, 8, 128, 48, 128).transpose(3, 2, 0, 1, 4).reshape(48, 128, 2048)
        if l % 2 == 0:
            ws[base:base + 26] = _to_blocks(ev_in[i][:, ecols])
            ws[base + 26:base + 34] = _to_blocks(ev_out[i])
        else:
            ws[base:base + 26] = _to_blocks(od_in[i][:, ocols])
            ws[base + 26:base + 34] = _to_blocks(od_out[i])
    con, cs = _host_constants()
    sinkb = np.ascontiguousarray(np.broadcast_to(f('a_sink').reshape(1, 16), (128, 16)))
    pm_base = np.zeros((384, 128), np.float32)
    pm_base[R_NG:R_NG + 64] = f('norm_g').reshape(64, 128)
    pm_base[R_AB:R_AB + 192] = f('ada_b').reshape(192, 128)
    pm_base[R_AQ:R_AQ + 2] = f('a_q_norm')
    pm_base[R_AK:R_AK + 2] = f('a_k_norm')
    pm_base[R_BC:R_BC + 48] = f('b_conv').reshape(48, 128)
    pm_base[R_CQ:R_CQ + 2] = f('c_q_norm')
    pm_base[R_CK:R_CK + 2] = f('c_k_norm')
    pm_base[R_CL:R_CL + 8] = f('c_lambda').reshape(8, 128)
    pm_base[R_CO:R_CO + 4] = f('c_out_norm').reshape(4, 128)
    pm_base[R_DQ:R_DQ + 2] = f('d_q_norm')
    pm_base[R_DK:R_DK + 2] = f('d_k_norm')
    pm_base[R_CC:R_CC + 16] = f('c_ctx').reshape(16, 128)
    xp, xs, c = f('x_prompt'), f('x_sample'), f('c')
    cak, cav, cck, ccv, cdk, cdv = (f(k) for k in ('cache_a_k', 'cache_a_v', 'cache_c_k', 'cache_c_v',
                                                   'cache_d_k', 'cache_d_v'))
    in_maps = []
    for i in range(8):
        pmi = pm_base.copy()
        pmi[R_CS:R_CS + 16] = c[i].reshape(16, 128)
        in_maps.append({
            "xp": np.ascontiguousarray(xp[4 * i:4 * i + 4].reshape(1024, 2048)),
            "xs": np.ascontiguousarray(xs[i]),
            "ws": ws, "wsa": wsa, "pm": pmi, "con": con, "cs": cs, "sinkb": sinkb,
            "cak": np.ascontiguousarray(cak[i]), "cav": np.ascontiguousarray(cav[i]),
            "cck": np.ascontiguousarray(cck[i]), "ccv": np.ascontiguousarray(ccv[i]),
            "cdk": np.ascontiguousarray(cdk[i]), "cdv": np.ascontiguousarray(cdv[i]),
        })
    if NL not in _CACHE:
        _CACHE[NL] = build_program(NL)
    nc = _CACHE[NL]
    ncores = int(os.environ.get("MK_CORES", "8"))
    res = run_bass_kernel_spmd(nc, in_maps[:ncores], core_ids=list(range(ncores)))
    R = list(res.results) + [res.results[0]] * (8 - ncores)
    cat = lambda k: np.concatenate([np.asarray(R[i][k]) for i in range(8)], axis=0)
    y_prompt = cat("yp").reshape(32, 256, 2048)
    y_sample = np.stack([np.asarray(R[i]["ys"]) for i in range(8)], axis=0)
    return (y_prompt, y_sample, cat("nak"), cat("nav"), cat("nck"), cat("ncv"), cat("ndk"), cat("ndv"))
```

```python
import math
import os
from contextlib import ExitStack

import numpy as np
import concourse.bass as bass
import concourse.mybir as mybir
from concourse.bass_utils import run_bass_kernel_spmd

F32 = mybir.dt.float32
BF16 = mybir.dt.bfloat16
AF = mybir.ActivationFunctionType
ALU = mybir.AluOpType

EPS = 1e-6
NBLK_L = 34
SQ128 = math.sqrt(128.0)


def I(m, *a, **k):
    return (m, a, k)


class Sched:
    ENG = ('pe', 'act', 'dve', 'pool', 'sp')

    def __init__(self, nc):
        self.nc = nc
        self.prog = {e: [] for e in self.ENG}
        self.cnt = {e: 0 for e in self.ENG}
        self.waited = {e: {} for e in self.ENG}
        self.res = {}
        self.slot_cnt = {}
        self.sem_names = list(self.ENG)
        self.final_tokens = []

    def _deps(self, eng, reads, writes):
        deps = []
        for k in reads:
            r = self.res.get(k)
            if r and r['w'] is not None:
                deps.append(('raw', r['w']))
            if r and isinstance(k, tuple) and k[0] == 'ps':
                for t in r['r']:
                    if t[0] != eng:
                        deps.append(('rar', t))
        for k in writes:
            r = self.res.get(k)
            if r:
                if r['w'] is not None:
                    deps.append(('waw', r['w']))
                for t in r['r']:
                    deps.append(('war', t))
        need = {}
        for kind, (sk, val) in deps:
            if sk == eng and eng == 'pe':
                continue
            if self.waited[eng].get(sk, 0) >= val:
                continue
            need[sk] = max(need.get(sk, 0), val)
        for sk, val in need.items():
            self.waited[eng][sk] = val
        return list(need.items())

    def _record(self, token, reads, writes):
        for k in reads:
            r = self.res.setdefault(k, {'w': None, 'r': []})
            r['r'].append(token)
        for k in writes:
            self.res[k] = {'w': token, 'r': []}

    def op(self, eng, ins, reads=(), writes=()):
        if isinstance(ins, tuple):
            ins = [ins]
        waits = self._deps(eng, reads, writes)
        self.cnt[eng] += 1
        token = (eng, self.cnt[eng])
        self.prog[eng].append((waits, ins, eng, 1, False))
        self._record(token, reads, writes)
        return token

    def dma(self, eng, slot, ins, reads=(), writes=(), final=False):
        if isinstance(ins, tuple):
            ins = [ins]
        sk = 'dma:' + str(slot)
        if sk not in self.slot_cnt:
            self.slot_cnt[sk] = 0
            self.sem_names.append(sk)
        waits = self._deps(eng, reads, writes)
        self.slot_cnt[sk] += 16 * len(ins)
        token = (sk, self.slot_cnt[sk])
        self.prog[eng].append((waits, ins, sk, 16, True))
        self._record(token, reads, writes)
        if final:
            self.final_tokens.append(token)
        return token

    def emit(self, stack):
        nc = self.nc
        sems = {}
        for name in self.sem_names:
            sems[name] = stack.enter_context(nc.semaphore(''.join(ch if ch.isalnum() else '_' for ch in name)))
        fin = {}
        for sk, val in self.final_tokens:
            fin[sk] = max(fin.get(sk, 0), val)
        block = stack.enter_context(nc.Block())
        engobj = {'pe': 'tensor', 'act': 'scalar', 'dve': 'vector', 'pool': 'gpsimd', 'sp': 'sync'}

        def make(ename):
            plist = self.prog[ename]

            def body(e):
                for waits, ins, sk, inc, is_dma in plist:
                    for wk, wv in waits:
                        e.wait_ge(sems[wk], wv)
                    last = None
                    for (m, a, k) in ins:
                        last = getattr(e, m)(*a, **k)
                        if is_dma:
                            last.then_inc(sems[sk], inc)
                    if not is_dma:
                        last.then_inc(sems[sk], inc)
                if ename == 'sp':
                    for wk, wv in fin.items():
                        e.wait_ge(sems[wk], wv)
            return body

        for ename in self.ENG:
            getattr(block, engobj[ename])(make(ename))


R_NG = 0
R_AB = 64
R_AQ = 256
R_AK = 258
R_BC = 260
R_CQ = 308
R_CK = 310
R_CL = 312
R_CO = 320
R_DQ = 324
R_DK = 326
R_CC = 328
R_CS = 344
D_AK, D_CK, D_DK, D_CO, D_NL = 0, 2, 4, 8, 12


def even_chunk_order():
    order = [8, 9, 10, 11]
    for j in range(4):
        order += [2 * j, 2 * j + 1, 12 + 2 * j, 13 + 2 * j]
    for m in range(8):
        order += [20 + m, 36 + m, 28 + m, 44 + m]
    return order


def odd_chunk_order():
    order = []
    for h in range(4):
        order += [8 + h, 12 + h, 16 + 2 * h, 17 + 2 * h, h, 4 + h, 24 + 2 * h, 25 + 2 * h]
    order += [40, 41, 42, 43]
    for j in range(4):
        order += [32 + 2 * j, 33 + 2 * j, 44 + 2 * j, 45 + 2 * j]
    return order


def build_program(NL=4):
    nc = bass.Bass("TRN2", target_bir_lowering=False)
    PHASE = int(os.environ.get("MK_PHASE", "5"))
    SKIP = set(os.environ.get("MK_SKIP", "").split(","))

    def din(name, shape):
        return nc.dram_tensor(name, list(shape), F32, kind="ExternalInput").ap()

    def dout(name, shape):
        return nc.dram_tensor(name, list(shape), F32, kind="ExternalOutput").ap()

    x_in = {'P': din("xp", [1024, 2048]), 'S': din("xs", [1024, 2048])}
    ws = din("ws", [4 * NBLK_L, 128, 4096])
    wsa = din("wsa", [4 * 48, 128, 2048])
    pm = din("pm", [384, 128])
    con_d = din("con", [128, 512])
    cs_d = din("cs", [128, 2048])
    sink_d = din("sinkb", [128, 16])
    cak = din("cak", [2, 2, 512, 128])
    cav = din("cav", [2, 2, 512, 128])
    cck = din("cck", [2, 2, 4, 512, 128])
    ccv = din("ccv", [2, 4, 512, 256])
    cdk = din("cdk", [2, 2, 512, 128])
    cdv = din("cdv", [2, 2, 512, 128])
    y_out = {'P': dout("yp", [1024, 2048]), 'S': dout("ys", [1024, 2048])}
    nak = dout("nak", [4, 2, 2, 256, 128])
    nav = dout("nav", [4, 2, 2, 256, 128])
    nck = dout("nck", [4, 2, 2, 4, 256, 128])
    ncv = dout("ncv", [4, 2, 4, 256, 256])
    ndk = dout("ndk", [4, 2, 2, 256, 128])
    ndv = dout("ndv", [4, 2, 2, 256, 128])

    with ExitStack() as st:
        def sb(name, shape, dt):
            return st.enter_context(nc.sbuf_tensor(name, list(shape), dt))

        xT = sb("xT", [128, 16, 1024], F32)
        hT = sb("hT", [128, 16, 1024], BF16)
        yT = sb("yT", [128, 16, 1024], BF16)
        wb = sb("wb", [128, 2, 4096], BF16)
        PT = sb("PT", [128, 384], F32)
        PMs = sb("PMs", [128, 3, 128], F32)
        CON = sb("CON", [128, 512], F32)
        CSN = sb("CSN", [128, 2048], F32)
        SKB = sb("SKB", [128, 16], F32)
        ES = sb("ES", [128, 16], F32)
        DS = sb("DS", [128, 16], F32)
        LT = sb("LT", [128, 8], F32)
        EPSC = sb("EPSC", [128, 4], F32)
        onesb = sb("onesb", [128, 128], BF16)
        onesf = sb("onesf", [128, 128], F32)
        sT = sb("sT", [128, 16, 2], BF16)
        modT = sb("modT", [128, 4, 48, 2], F32)
        MA = sb("MA", [128, 16], F32)
        wa = sb("wa", [128, 2048], BF16)
        KB = sb("KB", [128, 3072], BF16)
        VB = sb("VB", [128, 12, 256], BF16)
        QB = sb("QB", [128, 4, 1024], BF16)
        PTB = sb("PTB", [128, 4, 512], BF16)
        SQB = sb("SQB", [128, 2, 512], BF16)
        BT = sb("BT", [128, 1024], F32)
        ZP = sb("ZP", [128, 1032], F32)
        RSQ = sb("RSQ", [128, 512], F32)
        QG = sb("QG", [128, 512], F32)
        T1 = sb("T1", [128, 512], F32)
        T2 = sb("T2", [128, 512], F32)
        PS = [st.enter_context(nc.psum_tensor("ps%d" % i, [128, 512], F32)) for i in range(8)]

        ident = CON[:, 0:128]
        RT = CON[:, 128:256]
        maskA = CON[:, 256:384]
        maskB = CON[:, 384:512]
        COS = CSN[:, 0:1024]
        SIN = CSN[:, 1024:2048]

        S = Sched(nc)
        FUSE_SS = True
        SSB = [6, 7]
        state = {'gen': 0, 'sc': 0, 'ptb': 0, 'sq': 0, 'stg': 0, 'wi': 0, 'wdma': 0, 'genl': [0, 1, 2, 5, 6, 7],
                 'oalt': 0, 'tset': 0, 'tset_lock': False, 'ai': 0, 'adma': 0, 'pump': 0}

        def gen_bank():
            gl = state['genl']
            state['gen'] = (state['gen'] + 1) % len(gl)
            return gl[state['gen']]

        def sc_bank():
            b = 3 + state['sc']
            state['sc'] = (state['sc'] + 1) % 2
            return b

        def ptb_slot():
            s = state['ptb']
            state['ptb'] = (s + 1) % 4
            return s

        def sq_slot():
            s = state['sq']
            state['sq'] = (s + 1) % 2
            return s

        TSETS = [
            dict(RSQ=(RSQ[:, :], 'RSQ'), QG=(QG[:, :], 'QG'), T1=(T1[:, :], 'T1'), T2=(T2[:, :], 'T2')),
            dict(RSQ=(BT[:, 0:512], ('BT', 0)), QG=(BT[:, 512:1024], ('BT', 1)),
                 T1=(ZP[:, 0:512], ('ZP', 0)), T2=(ZP[:, 516:1028], ('ZP', 1))),
        ]

        def tset():
            if state['tset_lock']:
                return TSETS[0]
            state['tset'] ^= 1
            return TSETS[state['tset']]

        def stg():
            s = state['stg']
            state['stg'] = (s + 1) % 4
            return [TSETS[0]['T1'], TSETS[0]['T2'], TSETS[1]['T1'], TSETS[1]['T2']][s]

        def pk(b):
            return ('ps', b)

        def ACT(out, in_, func, reads, writes, **kw):
            S.op('act', I('activation', out=out, in_=in_, func=func, **kw), reads, writes)

        def TT(out, in0, in1, op, reads, writes, eng='dve'):
            S.op(eng, I('tensor_tensor', out=out, in0=in0, in1=in1, op=op), reads, writes)

        def TS(out, in0, s1, s2, op0, op1, reads, writes):
            if op1 is None:
                S.op('dve', I('tensor_scalar', out=out, in0=in0, scalar1=s1, scalar2=None, op0=op0), reads, writes)
            else:
                S.op('dve', I('tensor_scalar', out=out, in0=in0, scalar1=s1, scalar2=s2, op0=op0, op1=op1), reads, writes)

        def STT(out, in0, scalar, in1, op0, op1, reads, writes):
            S.op('dve', I('scalar_tensor_tensor', out=out, in0=in0, scalar=scalar, in1=in1, op0=op0, op1=op1),
                 reads, writes)

        def CPY(out, in_, reads, writes):
            S.op('dve', I('tensor_copy', out, in_), reads, writes)

        def TRANS4(bank, src_fn, reads):
            S.op('pe', [I('transpose', PS[bank][:, k * 128:(k + 1) * 128], src_fn(k), ident) for k in range(4)],
                 reads + ['CON'], [pk(bank)])

        wlist = []
        for g in ('P', 'S'):
            for l in range(NL):
                wlist += [l * NBLK_L + j for j in range(34)]
        NMOD = NL * 48

        def w_issue():
            i = state['wdma']
            if i >= len(wlist):
                return
            state['wdma'] += 1
            slot = i % 2
            S.dma('pool', 'w%d' % slot, I('dma_start', out=wb[:, slot, :], in_=ws[wlist[i]]), writes=[('wb', slot, 0), ('wb', slot, 1)])

        RING = 32
        hT2 = hT[:, :, :].rearrange("p a b -> p (a b)")
        yT2 = yT[:, :, :].rearrange("p a b -> p (a b)")

        def ring_slot(i):
            bs = (i // 4) % 4
            big, nm = (hT2, 'hT') if bs < 2 else (yT2, 'yT')
            half = bs % 2
            keys = [(nm, c, T) for c in range(8 * half, 8 * half + 8) for T in range(2)]
            return big, half * 8192, keys, bs

        def abuf(i):
            if i < RING:
                big, base, keys, _ = ring_slot(i)
                return big[:, base + (i % 4) * 2048:base + (i % 4 + 1) * 2048], keys
            return wa[:, :], [('wa', 0)]

        def a_issue():
            i = state['adma']
            if i >= NMOD:
                return
            if i < RING:
                state['adma'] += 4
                big, base, keys, bs = ring_slot(i)
                S.dma('pool', 'ring%d' % bs, I('dma_start', out=big[:, base:base + 8192].rearrange("p (c f) -> p c f", c=4),
                                               in_=wsa[i:i + 4].rearrange("c p f -> p c f")), writes=keys)
            else:
                state['adma'] += 1
                S.dma('pool', 'wa', I('dma_start', out=wa[:, :], in_=wsa[i]), writes=[('wa', 0)])

        def mod_pump(n):
            for _ in range(n):
                i = state['ai']
                if i >= NMOD:
                    return
                tgt = min(4 * (i // 4 + 3) + 3, RING - 1) if i < RING else i
                while state['adma'] <= min(tgt, NMOD - 1):
                    a_issue()
                state['ai'] += 1
                l, ch = i // 48, i % 48
                buf, keys = abuf(i)
                b = gen_bank()
                S.op('pe', [I('matmul', PS[b][:, 0:2], lhsT=buf[:, hf * 1024 + k8 * 128:hf * 1024 + (k8 + 1) * 128],
                              rhs=sT[:, 8 * hf + k8, :], start=(hf == 0 and k8 == 0), stop=(hf == 1 and k8 == 7))
                            for hf in range(2) for k8 in range(8)], keys + ['sT'], [pk(b)])
                TS(modT[:, l, ch, :], PS[b][:, 0:2], PT[:, R_AB + l * 48 + ch:R_AB + l * 48 + ch + 1],
                   None, ALU.add, None, [pk(b), 'PT'], [('modT', l, ch // 16)])
                if i >= RING - 1:
                    a_issue()

        def w_next():
            mod_pump(state['pump'])
            state['punit'] = 0
            i = state['wi']
            state['wi'] += 1
            while state['wdma'] <= min(i + 1, len(wlist) - 1):
                w_issue()
            return i % 2

        S.dma('sp', 'pm', I('dma_start', out=PMs[:, :, :], in_=pm.rearrange("(j p) d -> p j d", p=128)), writes=['PMs'])
        S.dma('sp', 'con', I('dma_start', out=CON[:, :], in_=con_d), writes=['CON'])
        S.dma('sp', 'csn', I('dma_start', out=CSN[:, :], in_=cs_d), writes=['CSN'])
        S.dma('sp', 'skb', I('dma_start', out=SKB[:, :], in_=sink_d), writes=['SKB'])
        for ei, ev in enumerate((2048.0 * EPS, 128.0 * EPS, 256.0 * EPS)):
            S.op('dve', I('memset', EPSC[:, ei:ei + 1], ev), writes=['EPSC'])
        S.op('dve', I('memset', onesb[:, :], 1.0), writes=['onesb'])
        S.op('dve', I('memset', onesf[:, :], 1.0), writes=['onesf'])
        S.op('dve', I('memset', ZP[:, :], 0.0), writes=[('ZP', 0), ('ZP', 1)])
        for j in range(3):
            b = gen_bank()
            S.op('pe', I('transpose', PS[b][:, 0:128], PMs[:, j, :], ident), ['PMs', 'CON'], [pk(b)])
            CPY(PT[:, j * 128:(j + 1) * 128], PS[b][:, 0:128], [pk(b)], ['PT'])
        ACT(ES[:, :], SKB[:, :], AF.Exp, ['SKB'], ['ES'])
        for cond in range(2):
            ACT(sT[:, :, cond], PT[:, R_CC + cond * 16:R_CC + cond * 16 + 16], AF.Silu, ['PT'], ['sT'])
        for (dcol, rcol) in ((D_AK, R_AK), (D_CK, R_CK), (D_DK, R_DK)):
            TS(DS[:, dcol:dcol + 2], PT[:, rcol:rcol + 2], SQ128, None, ALU.mult, None, ['PT'], ['DS'])
        lam_init = [0.8 - 0.6 * math.exp(-0.3 * (2 * i + 1)) for i in range(2)]
        for i in range(2):
            TS(DS[:, D_CO + 2 * i:D_CO + 2 * i + 2], PT[:, R_CO + 2 * i:R_CO + 2 * i + 2],
               16.0 * (1.0 - lam_init[i]), None, ALU.mult, None, ['PT'], ['DS'])
        for i in range(2):
            for r in range(2):
                c0 = R_CL + 4 * i + 2 * r
                TT(LT[:, 2 * i + r:2 * i + r + 1], PT[:, c0:c0 + 1], PT[:, c0 + 1:c0 + 2], ALU.mult, ['PT'], ['LT'])
        b = gen_bank()
        S.op('pe', I('matmul', PS[b][:, 0:4], lhsT=onesf[:, :], rhs=LT[:, 0:4], start=True, stop=True),
             ['onesf', 'LT'], [pk(b)])
        ACT(LT[:, 4:8], PS[b][:, 0:4], AF.Exp, [pk(b)], ['LT2'])
        for i in range(2):
            TT(DS[:, D_NL + i:D_NL + i + 1], LT[:, 5 + 2 * i:6 + 2 * i], LT[:, 4 + 2 * i:5 + 2 * i], ALU.subtract,
               ['LT2'], ['DS'])
            TS(DS[:, D_NL + i:D_NL + i + 1], DS[:, D_NL + i:D_NL + i + 1], -lam_init[i], None, ALU.add, None,
               ['DS'], ['DS'])

        def tsl(T):
            return slice(T * 512, (T + 1) * 512)

        pending = []

        def advance():
            for gsn in list(pending):
                try:
                    next(gsn)
                except StopIteration:
                    pending.remove(gsn)

        def drain():
            while pending:
                advance()

        def spawn(gsn):
            pending.append(gsn)
            try:
                next(gsn)
            except StopIteration:
                pending.remove(gsn)

        def proj_fm(slot, cj, T, src, srckey):
            state['punit'] = state.get('punit', 0) + 1
            if state['punit'] == 3:
                mod_pump(state['pump'])
            b = gen_bank()
            S.op('pe', [I('matmul', PS[b][:, :], lhsT=wb[:, slot, kc * 256 + cj * 128:kc * 256 + cj * 128 + 128],
                          rhs=src[:, kc, tsl(T)], start=(kc == 0), stop=(kc == 15)) for kc in range(16)],
                 [('wb', slot, 0), ('wb', slot, 1)] + [(srckey, c, T) for c in range(16)], [pk(b)])
            advance()
            return b

        def ones_sum(sq_aps, keys):
            b = gen_bank()
            n = len(sq_aps)
            for i, (ap, k) in enumerate(zip(sq_aps, keys)):
                S.op('pe', I('matmul', PS[b][:, :], lhsT=onesb[:, :], rhs=ap, start=(i == 0), stop=(i == n - 1)),
                     ['onesb', k], [pk(b)])
            return b

        def square_to(src_ap, src_keys):
            s = sq_slot()
            ACT(SQB[:, s, :], src_ap, AF.Square, src_keys, [('SQ', s)])
            return SQB[:, s, :], ('SQ', s)

        def rstd_from(bank, dest, destkey, epsn):
            ecol = {2048.0 * EPS: 0, 128.0 * EPS: 1, 256.0 * EPS: 2}[epsn]
            ACT(dest, PS[bank][:, :], AF.Ln, [pk(bank), 'EPSC'], [destkey], bias=EPSC[:, ecol:ecol + 1], scale=1.0)
            ACT(dest, dest, AF.Exp, [destkey], [destkey], scale=-0.5)

        def qk_epilogue(g, bank, T, gcol, dest, destkey, kout=None):
            spawn(qk_gen(g, bank, T, gcol, dest, destkey, kout))

        def qk_gen(g, bank, T, gcol, dest, destkey, kout):
            ts_ = tset()
            rsq, rsqk = ts_['RSQ']
            qg, qgk = ts_['QG']
            t1, t1k = ts_['T1']
            t2, t2k = ts_['T2']
            sl = sq_slot()
            ACT(SQB[:, sl, :], PS[bank][:, :], AF.Square, [pk(bank)], [('SQ', sl)])
            if g == 'S':
                ACT(qg, PS[bank][:, :], AF.Identity, [pk(bank), "PT", "DS"], [qgk], scale=gcol)
            else:
                TS(qg, PS[bank][:, :], gcol, None, ALU.mult, None, [pk(bank), "PT", "DS"], [qgk])
            yield
            sb_ = ones_sum([SQB[:, sl, :]], [('SQ', sl)])
            rstd_from(sb_, rsq, rsqk, 128.0 * EPS)
            if g == 'S':
                rb = gen_bank()
                S.op('pe', I('matmul', PS[rb][:, :], lhsT=RT, rhs=qg, start=True, stop=True), [qgk, 'CON'], [pk(rb)])
            yield
            if g == 'P':
                if kout is None:
                    TT(dest, qg, rsq, ALU.mult, [qgk, rsqk], [destkey])
                else:
                    TT(qg, qg, rsq, ALU.mult, [qgk, rsqk], [qgk])
                    ACT(dest, qg, AF.Copy, [qgk], [destkey])
                    tb = gen_bank()
                    TRANS4(tb, lambda k: qg[:, k * 128:(k + 1) * 128], [qgk])
                    ACT(t1, PS[tb][:, :], AF.Copy, [pk(tb)], [t1k])
                    if 'kout' not in SKIP:
                        S.dma('sp', 'st_' + str(t1k), [I('dma_start', out=kout[q].rearrange("(s p) d -> p s d", p=128),
                                                       in_=t1[:, q * 256:(q + 1) * 256].rearrange("p (s d) -> p s d", s=2))
                                                     for q in range(2)], reads=[t1k], final=True)
            else:
                TT(t1, qg, COS[:, tsl(T)], ALU.mult, [qgk, 'CSN'], [t1k])
                TT(t2, PS[rb][:, :], SIN[:, tsl(T)], ALU.mult, [pk(rb), 'CSN'], [t2k])
                TT(t1, t1, t2, ALU.add, [t1k, t2k], [t1k])
                TT(dest, t1, rsq, ALU.mult, [t1k, rsqk], [destkey])

        def load_ctx_k(src_list):
            if 'ctxk' in SKIP:
                return
            for (src, hm) in src_list:
                stage, sk = stg()
                S.dma('sp', 'ld_' + str(sk), I('dma_start', out=stage.rearrange("p (c d) -> p c d", d=128),
                                          in_=src.rearrange("(c p) d -> p c d", p=128)), writes=[sk])
                tb = gen_bank()
                TRANS4(tb, lambda k, stage=stage: stage[:, k * 128:(k + 1) * 128], [sk])
                ACT(KB[:, hm * 1536:hm * 1536 + 512], PS[tb][:, :], AF.Copy, [pk(tb)], [('KB', hm, 'ctx')])

        def load_ctx_v(src_list):
            if 'ctxv' in SKIP:
                return
            S.dma('pool', 'vbctx', [I('dma_start', out=VB[:, 0:4, c0:c0 + w], in_=src.rearrange("(c p) d -> p c d", p=128))
                                    for (src, c0, w) in src_list], writes=[('VB', 'ctx')])

        def proj_v(g, slot, vout_fn):
            c0 = 4 if g == 'S' else 0
            for u in range(4):
                b = gen_bank()
                ins = []
                for s2 in range(2):
                    ts_ = 2 * u + s2
                    for kc in range(16):
                        ins.append(I('matmul', PS[b][:, s2 * 256:(s2 + 1) * 256], lhsT=hT[:, kc, ts_ * 128:(ts_ + 1) * 128],
                                     rhs=wb[:, slot, kc * 256:(kc + 1) * 256], start=(s2 == 0 and kc == 0),
                                     stop=(kc == 15), skip_group_check=True))
                S.op('pe', ins, [('wb', slot, 0), ('wb', slot, 1)] + [('hT', c, u // 2) for c in range(16)], [pk(b)])
                advance()
                ACT(VB[:, c0 + 2 * u:c0 + 2 * u + 2, :], PS[b][:, :].rearrange("p (s v) -> p s v", s=2), AF.Copy,
                    [pk(b)], [('VB', 'own', u)])
                if g == 'P':
                    stage, sk = stg()
                    CPY(stage, PS[b][:, :], [pk(b)], [sk])
                    vout_fn(u, stage, sk)

        def attention(units, nv, q_of, evac):
            if 'attn' in SKIP:
                return
            drain()
            state['oalt'] ^= 1
            if nv == 1:
                state['dalt'] = state.get('dalt', 0) ^ 1
                den_b = [0, 1][state['dalt']]
                ob = [5, 7][state['oalt']:state['oalt'] + 1]
                sc_list = [3, 4, 2, 6]
            else:
                ob = [[5, 6], [7, 2]][state['oalt']]
                den_b = 0
                sc_list = [3, 4, 1]
            LA = len(sc_list) - 1
            sc = [None] * len(units)

            def subs_of(u):
                return u.get('subs') or [(u['k'], u['vkc'], u['lo'], u['hi'])]

            def scores(i):
                u = units[i]
                b = sc_list[i % len(sc_list)]
                sc[i] = b
                S.op('pe', [I('matmul', PS[b][:, plo:phi], lhsT=kap, rhs=q_of(u), start=(si == 0), stop=True,
                              skip_group_check=True) for si, (kap, _, plo, phi) in enumerate(subs_of(u))],
                     u['kkeys'] + u['qkeys'], [pk(b)])
            for i in range(min(LA, len(units))):
                scores(i)
            for i, u in enumerate(units):
                if i + LA < len(units):
                    scores(i + LA)
                b = sc[i]
                p = ptb_slot()
                lo, hi = u['lo'], u['hi']
                subs = subs_of(u)
                elo, ehi = min(x[2] for x in subs), max(x[3] for x in subs)
                ACT(PTB[:, p, elo:ehi], PS[b][:, elo:ehi], AF.Exp, [pk(b)], [('PTB', p)])
                for (col, mk) in u['masks']:
                    TT(PTB[:, p, col:col + 128], PTB[:, p, col:col + 128], mk, ALU.mult, [('PTB', p), 'CON'], [('PTB', p)])
                ins = []
                for si, (_, vkc, plo, phi) in enumerate(subs):
                    stf = (i == 0 and si == 0)
                    for j in range(nv):
                        ins.append(I('matmul', PS[ob[j]][:, lo:hi],
                                     lhsT=VB[:, vkc, u['vc0'] + j * 128:u['vc0'] + (j + 1) * 128],
                                     rhs=PTB[:, p, plo:phi], start=stf, stop=False, skip_group_check=True))
                    ins.append(I('matmul', PS[den_b][:, lo:hi], lhsT=onesb[:, :], rhs=PTB[:, p, plo:phi],
                                 start=stf, stop=False, skip_group_check=True))
                S.op('pe', ins, [('PTB', p), 'onesb'] + u['vkeys'], [pk(ob[j]) for j in range(nv)] + [pk(den_b)])
            evac(den_b, ob)

        def attention_pair(calls):
            if 'attn' in SKIP:
                return
            drain()
            sc_rot = [3, 4, 2, 6]
            items = []
            nmax = max(len(c['units']) for c in calls)
            for i in range(nmax):
                for k, c in enumerate(calls):
                    if i < len(c['units']):
                        items.append((k, i))
            assert len(items) <= 4
            scb, pslot = {}, {}
            for n_, (k, i) in enumerate(items):
                c = calls[k]
                u = c['units'][i]
                b = sc_rot[n_ % 4]
                scb[(k, i)] = b
                subs = u.get('subs') or [(u['k'], u['vkc'], u['lo'], u['hi'])]
                S.op('pe', [I('matmul', PS[b][:, plo:phi], lhsT=kap, rhs=c['q_of'](u), start=(si == 0), stop=True,
                              skip_group_check=True) for si, (kap, _, plo, phi) in enumerate(subs)],
                     u['kkeys'] + u['qkeys'], [pk(b)])
            for (k, i) in items:
                u = calls[k]['units'][i]
                b = scb[(k, i)]
                p = ptb_slot()
                pslot[(k, i)] = p
                subs = u.get('subs') or [(u['k'], u['vkc'], u['lo'], u['hi'])]
                elo, ehi = min(x[2] for x in subs), max(x[3] for x in subs)
                ACT(PTB[:, p, elo:ehi], PS[b][:, elo:ehi], AF.Exp, [pk(b)], [('PTB', p)])
            for (k, i) in items:
                u = calls[k]['units'][i]
                p = pslot[(k, i)]
                ob, den_b = [5, 7][k], [0, 1][k]
                lo, hi = u['lo'], u['hi']
                subs = u.get('subs') or [(u['k'], u['vkc'], u['lo'], u['hi'])]
                ins = []
                for si, (_, vkc, plo, phi) in enumerate(subs):
                    stf = (i == 0 and si == 0)
                    ins.append(I('matmul', PS[ob][:, lo:hi], lhsT=VB[:, vkc, u['vc0']:u['vc0'] + 128],
                                 rhs=PTB[:, p, plo:phi], start=stf, stop=False, skip_group_check=True))
                    ins.append(I('matmul', PS[den_b][:, lo:hi], lhsT=onesb[:, :], rhs=PTB[:, p, plo:phi],
                                 start=stf, stop=False, skip_group_check=True))
                S.op('pe', ins, [('PTB', p), 'onesb'] + u['vkeys'], [pk(ob), pk(den_b)])
            for k, c in enumerate(calls):
                c['evac']([0, 1][k], [[5, 7][k]])

        def dense_units(g, T, khm, vc0, qkeys):
            units = []
            if g == 'S':
                for kc in range(12):
                    kk = [('KB', khm, 'ctx')] if kc < 4 else [('KB', khm, (kc - 4) // 4)]
                    vk = [('VB', 'ctx')] if kc < 4 else [('VB', 'own', (kc - 4) // 2)]
                    units.append(dict(k=KB[:, khm * 1536 + kc * 128:khm * 1536 + (kc + 1) * 128], vkc=kc, vc0=vc0,
                                      lo=0, hi=512, masks=[], kkeys=kk, vkeys=vk, qkeys=qkeys, qlo=T * 512))
            else:
                for s2 in range(2):
                    seq = 2 * T + s2
                    subs = [(KB[:, khm * 1536 + (2 * seq + kc2) * 128:khm * 1536 + (2 * seq + kc2 + 1) * 128],
                             2 * seq + kc2, kc2 * 256, (kc2 + 1) * 256) for kc2 in range(2)]
                    units.append(dict(subs=subs, vc0=vc0, lo=s2 * 256, hi=(s2 + 1) * 256, masks=[],
                                      kkeys=[('KB', khm, (2 * seq) // 4)], vkeys=[('VB', 'own', seq)],
                                      qkeys=qkeys, qlo=T * 512 + s2 * 256))
            return units

        def band_units(T, khm, vc0, qkeys):
            units = []
            for kc in range(4):
                units.append(dict(k=KB[:, khm * 1536 + kc * 128:khm * 1536 + (kc + 1) * 128], vkc=kc, vc0=vc0,
                                  lo=0, hi=512, masks=[], kkeys=[('KB', khm, 'ctx')], vkeys=[('VB', 'ctx')],
                                  qkeys=qkeys, qlo=T * 512))
            for kb in range(max(0, 4 * T - 1), min(7, 4 * T + 4) + 1):
                qlo = max(kb - 1, 4 * T)
                qhi = min(kb + 1, 4 * T + 3)
                lo = (qlo - 4 * T) * 128
                hi = (qhi - 4 * T + 1) * 128
                masks = []
                if qlo <= kb + 1 <= qhi:
                    masks.append(((kb + 1 - 4 * T) * 128, maskA))
                if qlo <= kb - 1 <= qhi:
                    masks.append(((kb - 1 - 4 * T) * 128, maskB))
                units.append(dict(k=KB[:, khm * 1536 + 512 + kb * 128:khm * 1536 + 512 + (kb + 1) * 128],
                                  vkc=4 + kb, vc0=vc0, lo=lo, hi=hi, masks=masks,
                                  kkeys=[('KB', khm, kb // 4)], vkeys=[('VB', 'own', kb // 2)],
                                  qkeys=qkeys, qlo=T * 512 + lo))
            return units

        def gqa_mixer(g, l, kcache, vcache, gq_col, gk_col, ybase, nkout, nvout, sink_base, banded):
            state['genl'] = [0, 1, 2, 6]
            state['tset_lock'] = False
            li = l // 2
            koff = 512 if g == 'S' else 0
            slot = w_next()
            for T in range(2):
                for hm in range(2):
                    b = proj_fm(slot, hm, T, hT, 'hT')
                    kout = nkout[2 * T:2 * T + 2, li, hm] if g == 'P' else None
                    qk_epilogue(g, b, T, gk_col, KB[:, hm * 1536 + koff + T * 512:hm * 1536 + koff + (T + 1) * 512],
                                ('KB', hm, T), kout=kout)
            slot = w_next()
            if g == 'S':
                load_ctx_k([(kcache[li, hm], hm) for hm in range(2)])
                load_ctx_v([(vcache[li, hm], hm * 128, 128) for hm in range(2)])

            def vout(u, stage, sk):
                if 'vout' not in SKIP:
                  S.dma('sp', 'st_' + str(sk), [I('dma_start', out=nvout[u, li, hh].rearrange("(s p) d -> p s d", p=128),
                                           in_=stage.rearrange("p (s h d) -> p s h d", s=2, h=2)[:, :, hh, :])
                                         for hh in range(2)], reads=[sk], final=True)
            proj_v(g, slot, vout)
            for j in range(4):
                slot = w_next()
                qs = (j % 2) * 2
                for cj in range(2):
                    for T in range(2):
                        b = proj_fm(slot, cj, T, hT, 'hT')
                        qk_epilogue(g, b, T, gq_col, QB[:, qs + cj, tsl(T)], ('QB', qs + cj, T))
                slot = w_next()
                for cj in range(2):
                    for T in range(2):
                        b = proj_fm(slot, cj, T, hT, 'hT')
                        ych = ybase + 2 * j + cj
                        ACT(yT[:, ych, tsl(T)], PS[b][:, :], AF.Silu, [pk(b)], [('yT', ych, T)])
                for cj in range(2):
                    h = 2 * j + cj
                    khm = h // 4
                    qslot = qs + cj
                    pair = []
                    for T in range(2):
                        qkeys = [('QB', qslot, T)]
                        if banded:
                            units = band_units(T, khm, khm * 128, qkeys)
                        else:
                            units = dense_units(g, T, khm, khm * 128, qkeys)

                        def q_of(u, qslot=qslot):
                            return QB[:, qslot, u['qlo']:u['qlo'] + (u['hi'] - u['lo'])]

                        def evac(den_b, ob, h=h, T=T):
                            ts_ = tset()
                            rsq, rsqk = ts_['RSQ']
                            t1, t1k = ts_['T1']
                            if sink_base is not None:
                                ACT(rsq, PS[den_b][:, :], AF.Ln, [pk(den_b), 'ES'], [rsqk],
                                    bias=ES[:, sink_base + h:sink_base + h + 1], scale=1.0)
                            else:
                                ACT(rsq, PS[den_b][:, :], AF.Ln, [pk(den_b)], [rsqk])
                            ACT(rsq, rsq, AF.Exp, [rsqk], [rsqk], scale=-1.0)
                            TT(t1, PS[ob[0]][:, :], rsq, ALU.mult, [pk(ob[0]), rsqk], [t1k])
                            ych = ybase + h
                            TT(yT[:, ych, tsl(T)], t1, yT[:, ych, tsl(T)], ALU.mult, [t1k, ('yT', ych, T)],
                               [('yT', ych, T)])
                        if g == 'P':
                            pair.append(dict(units=units, q_of=q_of, evac=evac))
                        else:
                            attention(units, 1, q_of, evac)
                    if g == 'P':
                        attention_pair(pair)

        def conv_mixer(g, l):
            li = l // 2
            z4 = ZP[:, 0:1032].rearrange("p (s l) -> p s l", l=258)
            drain()
            state['genl'] = [0, 1, 2, 5, 6, 7]
            S.op('dve', I('memset', ZP[:, :], 0.0), [], [('ZP', 0), ('ZP', 1)])
            for m in range(8):
                slot = w_next()
                for T in range(2):
                    b = proj_fm(slot, 0, T, hT, 'hT')
                    ACT(BT[:, tsl(T)], PS[b][:, :], AF.Copy, [pk(b)], [('BT', T)])
                for T in range(2):
                    b = proj_fm(slot, 1, T, hT, 'hT')
                    if g == 'P':
                        TT(z4[:, 2 * T:2 * T + 2, 1:257], PS[b][:, :].rearrange("p (s l) -> p s l", s=2),
                           BT[:, tsl(T)].rearrange("p (s l) -> p s l", s=2), ALU.mult, [pk(b), ('BT', T)], [('ZP', T)])
                    else:
                        TT(ZP[:, 1 + T * 512:1 + (T + 1) * 512], PS[b][:, :], BT[:, tsl(T)], ALU.mult,
                           [pk(b), ('BT', T)], [('ZP', 0)] if T == 0 else [('ZP', 0), ('ZP', 1)])
                wc = [PT[:, R_BC + li * 24 + tap * 8 + m:R_BC + li * 24 + tap * 8 + m + 1] for tap in range(3)]
                for T in range(2):
                    if g == 'P':
                        zs = [z4[:, 2 * T:2 * T + 2, tap:tap + 256] for tap in range(3)]
                        acc = BT[:, tsl(T)].rearrange("p (s l) -> p s l", s=2)
                        zkeys = [('ZP', T)]
                    else:
                        zs = [ZP[:, T * 512 + tap:T * 512 + tap + 512] for tap in range(3)]
                        acc = BT[:, tsl(T)]
                        zkeys = [('ZP', 0), ('ZP', 1)]
                    ACT(acc, zs[0], AF.Identity, zkeys + ['PT'], [('BT', T)], scale=wc[0])
                    for tap in (1, 2):
                        STT(acc, zs[tap], wc[tap], acc, ALU.mult, ALU.add, zkeys + ['PT', ('BT', T)], [('BT', T)])
                slot = w_next()
                for T in range(2):
                    b = proj_fm(slot, 0, T, hT, 'hT')
                    TT(BT[:, tsl(T)], BT[:, tsl(T)], PS[b][:, :], ALU.mult, [pk(b), ('BT', T)], [('BT', T)])
                for T in range(2):
                    b = proj_fm(slot, 1, T, hT, 'hT')
                    ych = 8 + m
                    ACT(yT[:, ych, tsl(T)], PS[b][:, :], AF.Silu, [pk(b)], [('yT', ych, T)])
                    TT(yT[:, ych, tsl(T)], BT[:, tsl(T)], yT[:, ych, tsl(T)], ALU.mult, [('BT', T), ('yT', ych, T)],
                       [('yT', ych, T)])

        def diff_mixer(g, l):
            drain()
            state['genl'] = [0, 1, 2, 7]
            state['tset_lock'] = False
            li = l // 2
            koff = 512 if g == 'S' else 0
            gq_c = PT[:, R_CQ + li:R_CQ + li + 1]
            gk_c = DS[:, D_CK + li:D_CK + li + 1]
            for h in range(4):
                slot = w_next()
                for mp in range(2):
                    for T in range(2):
                        b = proj_fm(slot, mp, T, hT, 'hT')
                        kout = nck[2 * T:2 * T + 2, li, mp, h] if g == 'P' else None
                        qk_epilogue(g, b, T, gk_c, KB[:, mp * 1536 + koff + T * 512:mp * 1536 + koff + (T + 1) * 512],
                                    ('KB', mp, T), kout=kout)
                slot = w_next()
                if g == 'S':
                    load_ctx_k([(cck[li, mp, h], mp) for mp in range(2)])
                    load_ctx_v([(ccv[li, h], 0, 256)])

                def vout(u, stage, sk, h=h):
                    S.dma('sp', 'st_' + str(sk), I('dma_start', out=ncv[u, li, h].rearrange("(s p) v -> p s v", p=128),
                                              in_=stage.rearrange("p (s v) -> p s v", s=2)),
                          reads=[sk], final=True)
                proj_v(g, slot, vout)
                slot = w_next()
                qs = (h % 2) * 2
                for mp in range(2):
                    for T in range(2):
                        b = proj_fm(slot, mp, T, hT, 'hT')
                        qk_epilogue(g, b, T, gq_c, QB[:, qs + mp, tsl(T)], ('QB', qs + mp, T))
                slot = w_next()
                for cj in range(2):
                    for T in range(2):
                        b = proj_fm(slot, cj, T, hT, 'hT')
                        ych = 2 * h + cj
                        ACT(yT[:, ych, tsl(T)], PS[b][:, :], AF.Silu, [pk(b)], [('yT', ych, T)])
                AOB = [[(BT[:, 0:512], ('BT', 0)), (BT[:, 512:1024], ('BT', 1))],
                       [(ZP[:, 0:512], ('ZP', 0)), (ZP[:, 516:1028], ('ZP', 1))]]

                def run_attn(T, mp, h=h, qs=qs):
                    units = dense_units(g, T, mp, 0, [('QB', qs + mp, T)])

                    def q_of(u, qslot=qs + mp):
                        return QB[:, qslot, u['qlo']:u['qlo'] + (u['hi'] - u['lo'])]

                    def evac(den_b, ob):
                        ACT(RSQ[:, :], PS[den_b][:, :], AF.Ln, [pk(den_b)], ['RSQ'])
                        ACT(RSQ[:, :], RSQ[:, :], AF.Exp, ['RSQ'], ['RSQ'], scale=-1.0)
                        if mp == 0:
                            for jv in range(2):
                                ao, aok = AOB[T][jv]
                                TT(ao, PS[ob[jv]][:, :], RSQ[:, :], ALU.mult, [pk(ob[jv]), 'RSQ'], [aok])
                        else:
                            TS(RSQ[:, :], RSQ[:, :], DS[:, D_NL + li:D_NL + li + 1], None, ALU.mult, None,
                               ['RSQ', 'DS'], ['RSQ'])
                            for jv in range(2):
                                ao, aok = AOB[T][jv]
                                TT(T1[:, :], PS[ob[jv]][:, :], RSQ[:, :], ALU.mult, [pk(ob[jv]), 'RSQ'], ['T1'])
                                TT(ao, ao, T1[:, :], ALU.add, ['T1', aok], [aok])
                    attention(units, 2, q_of, evac)

                def final_gen(T, h=h):
                    sqa, sqk = [], []
                    for jv in range(2):
                        ao, aok = AOB[T][jv]
                        a_, k_ = square_to(ao, [aok])
                        sqa.append(a_)
                        sqk.append(k_)
                    yield
                    sbk = ones_sum(sqa, sqk)
                    rstd_from(sbk, RSQ[:, :], 'RSQ', 256.0 * EPS)
                    for jv in range(2):
                        ao, aok = AOB[T][jv]
                        ych = 2 * h + jv
                        STT(T1[:, :], ao, DS[:, D_CO + 2 * li + jv:D_CO + 2 * li + jv + 1],
                            RSQ[:, :], ALU.mult, ALU.mult, [aok, 'DS', 'RSQ'], ['T1'])
                        TT(yT[:, ych, tsl(T)], T1[:, :], yT[:, ych, tsl(T)], ALU.mult, ['T1', ('yT', ych, T)],
                           [('yT', ych, T)])

                run_attn(0, 0)
                run_attn(0, 1)
                run_attn(1, 0)
                for _ in final_gen(0):
                    pass
                run_attn(1, 1)
                spawn(final_gen(1))

        for g in ('P', 'S'):
            cond = 0 if g == 'P' else 1
            state['genl'] = [0, 1, 2, 5, 6, 7]
            if g == 'P':
                while state['adma'] < min(16, NMOD):
                    a_issue()
            for ts_ in range(8):
                for cb in range(4):
                    if g == 'P' and ts_ * 4 + cb >= 8:
                        mod_pump(2 if state['ai'] < RING else 1)
                    stage, sk = stg()
                    S.dma('sp', 'ld_' + str(sk), I('dma_start', out=stage,
                                              in_=x_in[g][ts_ * 128:(ts_ + 1) * 128, cb * 512:(cb + 1) * 512]), writes=[sk])
                    tb = gen_bank()
                    TRANS4(tb, lambda k, stage=stage: stage[:, k * 128:(k + 1) * 128], [sk])
                    outap = xT[:, cb * 4:(cb + 1) * 4, ts_ * 128:(ts_ + 1) * 128]
                    inap = PS[tb][:, :].rearrange("p (k t) -> p k t", k=4)
                    wr = [('xT', c, ts_ // 4) for c in range(cb * 4, cb * 4 + 4)]
                    if cb % 2 == 0:
                        ACT(outap, inap, AF.Copy, [pk(tb)], wr)
                    else:
                        CPY(outap, inap, [pk(tb)], wr)

            for l in range(NL):
                li = l // 2
                even = (l % 2 == 0)
                if g == 'P':
                    state['pump'] = 0
                    mod_pump(max(0, l * 48 + 32 - state['ai']))
                    state['pump'] = 1
                if PHASE < 2:
                    continue
                TS(MA[:, :], modT[:, l, 16:32, cond], 1.0, math.sqrt(2048.0), ALU.add, ALU.mult, [('modT', l, 1)], ['MA'])
                TT(MA[:, :], MA[:, :], PT[:, R_NG + l * 16:R_NG + l * 16 + 16], ALU.mult, ['MA', 'PT'], ['MA'])
                for T in range(2):
                    if l > 0 and FUSE_SS:
                        b = SSB[T]
                    else:
                        b = gen_bank()
                        for c in range(16):
                            s = sq_slot()
                            if c % 2 == 0:
                                ACT(SQB[:, s, :], xT[:, c, tsl(T)], AF.Square, [('xT', c, T)], [('SQ', s)])
                            else:
                                TT(SQB[:, s, :], xT[:, c, tsl(T)], xT[:, c, tsl(T)], ALU.mult, [('xT', c, T)], [('SQ', s)])
                            S.op('pe', I('matmul', PS[b][:, :], lhsT=onesb[:, :], rhs=SQB[:, s, :], start=(c == 0), stop=(c == 15)),
                                 ['onesb', ('SQ', s)], [pk(b)])
                    rstd_from(b, RSQ[:, :], 'RSQ', 2048.0 * EPS)
                    for c in range(16):
                        stage, sk = stg()
                        STT(stage, xT[:, c, tsl(T)], MA[:, c:c + 1], RSQ[:, :], ALU.mult, ALU.mult,
                            [('xT', c, T), 'MA', 'RSQ'], [sk])
                        ACT(hT[:, c, tsl(T)], stage, AF.Identity, [sk, ('modT', l, 0)], [('hT', c, T)],
                            bias=modT[:, l, c, cond:cond + 1], scale=1.0)

                if PHASE < 3:
                    continue
                if even:
                    gqa_mixer(g, l, cak, cav, PT[:, R_AQ + li:R_AQ + li + 1], DS[:, D_AK + li:D_AK + li + 1], 0,
                              nak, nav, 8 * li, banded=(g == 'S'))
                    if PHASE >= 4:
                        conv_mixer(g, l)
                else:
                    diff_mixer(g, l)
                    gqa_mixer(g, l, cdk, cdv, PT[:, R_DQ + li:R_DQ + li + 1], DS[:, D_DK + li:D_DK + li + 1], 8,
                              ndk, ndv, None, banded=False)

                drain()
                if PHASE < 5:
                    continue
                fuse = FUSE_SS and (l + 1 < NL)
                state['genl'] = [0, 1, 2, 5] if fuse else [0, 1, 2, 5, 6, 7]
                state['tset_lock'] = False
                if g == 'P':
                    mod_pump(max(0, (l + 1) * 48 - state['ai']))

                def ss_gen(mch, T):
                    sl = sq_slot()
                    if mch % 2 == 0:
                        ACT(SQB[:, sl, :], xT[:, mch, tsl(T)], AF.Square, [('xT', mch, T)], [('SQ', sl)])
                    else:
                        TT(SQB[:, sl, :], xT[:, mch, tsl(T)], xT[:, mch, tsl(T)], ALU.mult, [('xT', mch, T)], [('SQ', sl)])
                    yield
                    S.op('pe', I('matmul', PS[SSB[T]][:, :], lhsT=onesb[:, :], rhs=SQB[:, sl, :],
                                 start=(mch == 0), stop=(mch == 15), skip_group_check=True),
                         ['onesb', ('SQ', sl)], [pk(SSB[T])])

                for jb in range(8):
                    slot = w_next()
                    for cj in range(2):
                        mch = 2 * jb + cj
                        for T in range(2):
                            b = proj_fm(slot, cj, T, yT, 'yT')
                            STT(xT[:, mch, tsl(T)], PS[b][:, :], modT[:, l, 32 + mch, cond:cond + 1], xT[:, mch, tsl(T)],
                                ALU.mult, ALU.add, [pk(b), ('modT', l, 2), ('xT', mch, T)], [('xT', mch, T)])
                            if fuse:
                                spawn(ss_gen(mch, T))
                drain()

            for ts_ in range(8):
                for cb in range(4):
                    tb = gen_bank()
                    TRANS4(tb, lambda k, ts_=ts_, cb=cb: xT[:, cb * 4 + k, ts_ * 128:(ts_ + 1) * 128],
                           [('xT', cb * 4 + k, ts_ // 4) for k in range(4)])
                    stage, sk = stg()
                    if cb % 2 == 0:
                        ACT(stage, PS[tb][:, :], AF.Copy, [pk(tb)], [sk])
                    else:
                        CPY(stage, PS[tb][:, :], [pk(tb)], [sk])
                    S.dma('sp', 'st_' + str(sk), I('dma_start', out=y_out[g][ts_ * 128:(ts_ + 1) * 128, cb * 512:(cb + 1) * 512],
                                              in_=stage), reads=[sk], final=True)

        S.emit(st)
    return nc


def _to_blocks(W):
    nb = W.shape[1] // 256
    return np.ascontiguousarray(W.reshape(16, 128, nb, 256).transpose(2, 1, 0, 3)).reshape(nb, 128, 4096)


def _chunk_cols(order):
    return np.concatenate([np.arange(c * 128, (c + 1) * 128) for c in order])


def _host_constants():
    con = np.zeros((128, 512), np.float32)
    con[:, 0:128] = np.eye(128, dtype=np.float32)
    R = np.zeros((128, 128), np.float32)
    for p in range(128):
        if p % 64 < 32:
            R[p, p + 32] = -1.0
        else:
            R[p, p - 32] = 1.0
    con[:, 128:256] = R.T
    jj = np.arange(128)[:, None]
    ii = np.arange(128)[None, :]
    con[:, 256:384] = (jj >= ii).astype(np.float32)
    con[:, 384:512] = (jj <= ii).astype(np.float32)
    t = np.arange(1024)
    row = (t // 64).astype(np.float32)
    col = (t % 64).astype(np.float32)
    inv = (np.float32(10000.0) ** (-np.arange(32, dtype=np.float32) / np.float32(32))).astype(np.float32)
    cs = np.zeros((128, 2048), np.float32)
    for p in range(128):
        pos = row if p < 64 else col
        ang = (pos * inv[p % 32]).astype(np.float32)
        cs[p, 0:1024] = np.cos(ang)
        cs[p, 1024:2048] = np.sin(ang)
    return con, cs


_CACHE = {}


def kernel(**inp):
    NL = int(os.environ.get("MK_NL", "4"))
    f = lambda k: np.asarray(inp[k], dtype=np.float32)
    ws = np.empty((4 * NBLK_L, 128, 4096), np.float32)
    wsa = np.empty((4 * 48, 128, 2048), np.float32)
    ada_w = f('ada_w')
    ev_in, ev_out, od_in, od_out = f('ev_w_in'), f('ev_w_out'), f('od_w_in'), f('od_w_out')
    ecols, ocols = _chunk_cols(even_chunk_order()), _chunk_cols(odd_chunk_order())
    for l in range(4):
        i = l // 2
        base = l * NBLK_L
        wsa[l * 48:(l + 1) * 48] = ada_w[l].reshape(2, 8, 128, 48, 128).transpose(3, 2, 0, 1, 4).reshape(48, 128, 2048)
        if l % 2 == 0:
            ws[base:base + 26] = _to_blocks(ev_in[i][:, ecols])
            ws[base + 26:base + 34] = _to_blocks(ev_out[i])
        else:
            ws[base:base + 26] = _to_blocks(od_in[i][:, ocols])
            ws[base + 26:base + 34] = _to_blocks(od_out[i])
    con, cs = _host_constants()
    sinkb = np.ascontiguousarray(np.broadcast_to(f('a_sink').reshape(1, 16), (128, 16)))
    pm_base = np.zeros((384, 128), np.float32)
    pm_base[R_NG:R_NG + 64] = f('norm_g').reshape(64, 128)
    pm_base[R_AB:R_AB + 192] = f('ada_b').reshape(192, 128)
    pm_base[R_AQ:R_AQ + 2] = f('a_q_norm')
    pm_base[R_AK:R_AK + 2] = f('a_k_norm')
    pm_base[R_BC:R_BC + 48] = f('b_conv').reshape(48, 128)
    pm_base[R_CQ:R_CQ + 2] = f('c_q_norm')
    pm_base[R_CK:R_CK + 2] = f('c_k_norm')
    pm_base[R_CL:R_CL + 8] = f('c_lambda').reshape(8, 128)
    pm_base[R_CO:R_CO + 4] = f('c_out_norm').reshape(4, 128)
    pm_base[R_DQ:R_DQ + 2] = f('d_q_norm')
    pm_base[R_DK:R_DK + 2] = f('d_k_norm')
    pm_base[R_CC:R_CC + 16] = f('c_ctx').reshape(16, 128)
    xp, xs, c = f('x_prompt'), f('x_sample'), f('c')
    cak, cav, cck, ccv, cdk, cdv = (f(k) for k in ('cache_a_k', 'cache_a_v', 'cache_c_k', 'cache_c_v',
                                                   'cache_d_k', 'cache_d_v'))
    in_maps = []
    for i in range(8):
        pmi = pm_base.copy()
        pmi[R_CS:R_CS + 16] = c[i].reshape(16, 128)
        in_maps.append({
            "xp": np.ascontiguousarray(xp[4 * i:4 * i + 4].reshape(1024, 2048)),
            "xs": np.ascontiguousarray(xs[i]),
            "ws": ws, "wsa": wsa, "pm": pmi, "con": con, "cs": cs, "sinkb": sinkb,
            "cak": np.ascontiguousarray(cak[i]), "cav": np.ascontiguousarray(cav[i]),
            "cck": np.ascontiguousarray(cck[i]), "ccv": np.ascontiguousarray(ccv[i]),
            "cdk": np.ascontiguousarray(cdk[i]), "cdv": np.ascontiguousarray(cdv[i]),
        })
    if NL not in _CACHE:
        _CACHE[NL] = build_program(NL)
    nc = _CACHE[NL]
    ncores = int(os.environ.get("MK_CORES", "8"))
    res = run_bass_kernel_spmd(nc, in_maps[:ncores], core_ids=list(range(ncores)))
    R = list(res.results) + [res.results[0]] * (8 - ncores)
    cat = lambda k: np.concatenate([np.asarray(R[i][k]) for i in range(8)], axis=0)
    y_prompt = cat("yp").reshape(32, 256, 2048)
    y_sample = np.stack([np.asarray(R[i]["ys"]) for i in range(8)], axis=0)
    return (y_prompt, y_sample, cat("nak"), cat("nav"), cat("nck"), cat("ncv"), cat("ndk"), cat("ndv"))
```
